# Optimizing a Trainium2 kernel written in Bass

```python
import jax, jax.numpy as jnp
from jax import lax
import numpy as np

D_MODEL = 1024
BATCH = 16
SEQ = 4096
DEPTH = 4

N_MIXERS = 2
N_RET_LAYERS = (DEPTH + 1) // 2
N_CONV_LAYERS = DEPTH // 2
RET_HEADS = 4
RET_QK_DIM = D_MODEL // RET_HEADS
RET_V_DIM = 2 * RET_QK_DIM
RET_Q_W = RET_HEADS * RET_QK_DIM
RET_VAL_W = RET_HEADS * RET_V_DIM
RET_IN_W = 2 * RET_Q_W + 2 * RET_VAL_W
RET_CHUNK = 128
ROPE_BASE = 10000.0
CONV_WIDTH = 3
D_FF = ((8 * D_MODEL // 3 + 255) // 256) * 256
N_SUBLAYERS = 3
N_MOD = 3
EPS = 1e-6

kernel_name = "hybrid_retention_shortconv_macaron_adaln"


def rmsnorm(x, g):
    xf = x.astype(jnp.float32)
    xf = xf * lax.rsqrt(jnp.mean(xf * xf, axis=-1, keepdims=True) + EPS)
    return xf.astype(x.dtype) * g


def modulate(x, g, shift, scale):
    return rmsnorm(x, g) * (1 + scale[:, None, :]) + shift[:, None, :]


def swiglu(h, w_in, w_out):
    gu = h @ w_in
    gate, up = gu[..., :D_FF], gu[..., D_FF:]
    return (jax.nn.silu(gate) * up) @ w_out


def rotary(x, pos):
    d = x.shape[-1]
    half = d // 2
    inv_freq = ROPE_BASE ** (-jnp.arange(half, dtype=jnp.float32) * 2.0 / d)
    ang = pos[:, None] * inv_freq[None, :]
    cos = jnp.cos(ang)[None, :, None, :].astype(x.dtype)
    sin = jnp.sin(ang)[None, :, None, :].astype(x.dtype)
    x1, x2 = x[..., :half], x[..., half:]
    return jnp.concatenate([x1 * cos - x2 * sin, x2 * cos + x1 * sin], axis=-1)


def retention(h, w_in, gn_g, w_out):
    B, S, _ = h.shape
    H, dk, dv, C = RET_HEADS, RET_QK_DIM, RET_V_DIM, RET_CHUNK
    nc = S // C
    proj = h @ w_in
    q = proj[..., :RET_Q_W].reshape(B, S, H, dk)
    k = proj[..., RET_Q_W:2 * RET_Q_W].reshape(B, S, H, dk)
    v = proj[..., 2 * RET_Q_W:2 * RET_Q_W + RET_VAL_W].reshape(B, S, H, dv)
    g = proj[..., 2 * RET_Q_W + RET_VAL_W:]

    pos = jnp.arange(S, dtype=jnp.float32)
    q = rotary(q, pos)
    k = rotary(k, pos) * (dk ** -0.5)

    log_gamma = jnp.log(1.0 - 2.0 ** (-5.0 - jnp.arange(H, dtype=jnp.float32)))
    idx = jnp.arange(C, dtype=jnp.float32)
    diff = idx[:, None] - idx[None, :]
    decay_mask = jnp.where(diff >= 0,
                           jnp.exp(log_gamma[:, None, None] * jnp.maximum(diff, 0.0)[None]),
                           0.0).astype(h.dtype)
    q_decay = jnp.exp(log_gamma[None, :] * (idx[:, None] + 1.0)).astype(h.dtype)
    k_decay = jnp.exp(log_gamma[None, :] * (C - 1.0 - idx[:, None])).astype(h.dtype)
    chunk_decay = jnp.exp(log_gamma * C).astype(h.dtype)

    qc = q.reshape(B, nc, C, H, dk)
    kc = k.reshape(B, nc, C, H, dk)
    vc = v.reshape(B, nc, C, H, dv)

    scores = jnp.einsum('bnihd,bnjhd->bnhij', qc, kc) * decay_mask[None, None]
    intra = jnp.einsum('bnhij,bnjhv->bnihv', scores, vc)

    def step(state, xs):
        q_n, k_n, v_n = xs
        inter_n = jnp.einsum('bihd,bhdv->bihv', q_n * q_decay[None, :, :, None], state)
        state = state * chunk_decay[None, :, None, None] + jnp.einsum(
            'bjhd,bjhv->bhdv', k_n * k_decay[None, :, :, None], v_n)
        return state, inter_n

    state0 = jnp.zeros((B, H, dk, dv), dtype=h.dtype)
    _, inter = lax.scan(step, state0, (jnp.moveaxis(qc, 1, 0),
                                       jnp.moveaxis(kc, 1, 0),
                                       jnp.moveaxis(vc, 1, 0)))
    inter = jnp.moveaxis(inter, 0, 1)

    o = (intra + inter).reshape(B, S, H, dv)
    o = rmsnorm(o, gn_g.reshape(H, dv))
    return (jax.nn.silu(g) * o.reshape(B, S, H * dv)) @ w_out


def short_conv(h, w_in, conv_w, w_out):
    S = h.shape[1]
    proj = h @ w_in
    gate_b = proj[..., :D_MODEL]
    gate_c = proj[..., D_MODEL:2 * D_MODEL]
    u = proj[..., 2 * D_MODEL:]
    z = gate_c * u
    zp = jnp.pad(z, ((0, 0), (CONV_WIDTH - 1, 0), (0, 0)))
    conv = zp[:, 0:S] * conv_w[0]
    for tap in range(1, CONV_WIDTH):
        conv = conv + zp[:, tap:tap + S] * conv_w[tap]
    return (gate_b * conv) @ w_out


def setup_inputs(seed: int = 0) -> dict:
    key = jax.random.key(seed)
    ks = jax.random.split(key, 16)
    D = D_MODEL
    f32 = jnp.float32
    x = jax.random.normal(ks[0], (BATCH, SEQ, D), f32)
    c = jax.random.normal(ks[1], (BATCH, D), f32)
    ada_w = jax.random.normal(ks[2], (DEPTH, D, N_SUBLAYERS * N_MOD * D), f32) * (0.1 * D ** -0.5)
    offset = jnp.array([0.0, 0.0, 1.0], f32)[None, None, :, None]
    ada_b = (0.02 * jax.random.normal(ks[3], (DEPTH, N_SUBLAYERS, N_MOD, D), f32) + offset
             ).reshape(DEPTH, N_SUBLAYERS * N_MOD * D)
    norm_g = 1.0 + 0.02 * jax.random.normal(ks[4], (DEPTH, N_SUBLAYERS, D), f32)
    ffn_w_in = jax.random.normal(ks[5], (DEPTH, 2, D, 2 * D_FF), f32) * D ** -0.5
    ffn_w_out = jax.random.normal(ks[6], (DEPTH, 2, D_FF, D), f32) * D_FF ** -0.5
    ret_w_in = jax.random.normal(ks[7], (N_RET_LAYERS, D, RET_IN_W), f32) * D ** -0.5
    ret_gn_g = 1.0 + 0.02 * jax.random.normal(ks[8], (N_RET_LAYERS, RET_VAL_W), f32)
    ret_w_out = jax.random.normal(ks[9], (N_RET_LAYERS, RET_VAL_W, D), f32) * RET_VAL_W ** -0.5
    conv_w_in = jax.random.normal(ks[10], (N_CONV_LAYERS, D, 3 * D), f32) * D ** -0.5
    conv_w = jax.random.normal(ks[11], (N_CONV_LAYERS, CONV_WIDTH, D), f32) * CONV_WIDTH ** -0.5
    conv_w_out = jax.random.normal(ks[12], (N_CONV_LAYERS, D, D), f32) * D ** -0.5
    final_g = 1.0 + 0.02 * jax.random.normal(ks[13], (D,), f32)
    return {"x": x, "c": c, "ada_w": ada_w, "ada_b": ada_b, "norm_g": norm_g,
            "ffn_w_in": ffn_w_in, "ffn_w_out": ffn_w_out,
            "ret_w_in": ret_w_in, "ret_gn_g": ret_gn_g, "ret_w_out": ret_w_out,
            "conv_w_in": conv_w_in, "conv_w": conv_w, "conv_w_out": conv_w_out,
            "final_g": final_g}


def reference(x, c, ada_w, ada_b, norm_g, ffn_w_in, ffn_w_out,
              ret_w_in, ret_gn_g, ret_w_out, conv_w_in, conv_w, conv_w_out, final_g):
    B = x.shape[0]
    cond = jax.nn.silu(c)
    for l in range(DEPTH):
        mod = (cond @ ada_w[l] + ada_b[l]).reshape(B, N_SUBLAYERS, N_MOD, D_MODEL)
        shift, scale, gate = mod[:, :, 0], mod[:, :, 1], mod[:, :, 2]

        h = modulate(x, norm_g[l, 0], shift[:, 0], scale[:, 0])
        x = x + 0.5 * gate[:, 0, None, :] * swiglu(h, ffn_w_in[l, 0], ffn_w_out[l, 0])

        h = modulate(x, norm_g[l, 1], shift[:, 1], scale[:, 1])
        j = l // N_MIXERS
        if l % N_MIXERS == 0:
            y = retention(h, ret_w_in[j], ret_gn_g[j], ret_w_out[j])
        else:
            y = short_conv(h, conv_w_in[j], conv_w[j], conv_w_out[j])
        x = x + gate[:, 1, None, :] * y

        h = modulate(x, norm_g[l, 2], shift[:, 2], scale[:, 2])
        x = x + 0.5 * gate[:, 2, None, :] * swiglu(h, ffn_w_in[l, 1], ffn_w_out[l, 1])
    return rmsnorm(x, final_g)
```

```python
import os
import numpy as np
from contextlib import ExitStack

import concourse.bass as bass
import concourse.mybir as mybir
from concourse.bass_utils import run_bass_kernel_spmd

F32 = mybir.dt.float32
BF16 = mybir.dt.bfloat16
F32R = mybir.dt.float32r
AF = mybir.ActivationFunctionType
ALU = mybir.AluOpType

D = 1024
KC = 8
DFF = 2816
NF = 22
NH = 4
DK = 256
DV = 512
T = 512
DEPTH = 4
EPS = 1e-6
SEQ = 4096
NCORES = 8
GAMMA = [1.0 - 2.0 ** (-5.0 - h) for h in range(NH)]
NSLOT = 5
SLOT_ELEMS = 4096
NTAB = 2048 + 2048 + 16 + 128
OUT_ON_ACT = os.environ.get('K_OUT_ACT', '1') == '1'
X_PREFETCH = os.environ.get('K_PREF', '1') == '1'

R_C = 0
R_NG = 16
R_FG = 112
R_CW = 120
R_GN = 168
R_AB = 200
NVROW = 512


class Eng:
    def __init__(self, h, sem):
        self.h = h
        self.sem = sem
        self.cnt = 0
        self.seen = {}
        self.nwait = 0

    def wait(self, evs):
        for sem, val in evs.items():
            if self.seen.get(sem, 0) < val:
                self.h.wait_ge(sem, val)
                self.seen[sem] = val
                self.nwait += 1

    def sig(self, inst):
        self.cnt += 1
        inst.then_inc(self.sem, 1)
        return (self.sem, self.cnt)


class Buf:
    __slots__ = ("ap", "w", "r")

    def __init__(self, ap, pend=None):
        self.ap = ap
        self.w = {}
        self.r = dict(pend) if pend else {}


def _mg(d, s, v):
    if d.get(s, 0) < v:
        d[s] = v


def _gather(reads, writes):
    evs = {}
    for b in reads:
        for s, v in b.w.items():
            _mg(evs, s, v)
    for b in writes:
        for s, v in b.w.items():
            _mg(evs, s, v)
        for s, v in b.r.items():
            _mg(evs, s, v)
    return evs


def _commit(ev, reads, writes):
    for b in reads:
        _mg(b.r, ev[0], ev[1])
    for b in writes:
        b.w = {ev[0]: ev[1]}
        b.r = {}


def op(eng, fn, reads=(), writes=()):
    eng.wait(_gather(reads, writes))
    inst = fn()
    ev = eng.sig(inst)
    _commit(ev, reads, writes)
    return ev


class Region:
    def __init__(self, base_f32):
        self.base = base_f32
        self.bufs = []
        self.pend = {}

    def sync(self):
        for b in self.bufs:
            for s, v in b.w.items():
                _mg(self.pend, s, v)
            for s, v in b.r.items():
                _mg(self.pend, s, v)

    def reset(self):
        self.sync()
        self.bufs = []

    def ap(self, off, shape, dt):
        n = int(np.prod(shape))
        nbytes = n * (4 if dt == F32 else 2)
        assert off % 4 == 0 and nbytes % 4 == 0
        a = self.base[:, off // 4:(off + nbytes) // 4]
        if dt == BF16:
            a = a.bitcast(BF16)
        if len(shape) == 2:
            a = a.rearrange("p (a b) -> p a b", a=shape[0])
        elif len(shape) == 3:
            a = a.rearrange("p (a b c) -> p a b c", a=shape[0], b=shape[1])
        return a

    def buf(self, off, shape, dt):
        b = Buf(self.ap(off, shape, dt), self.pend)
        self.bufs.append(b)
        return b


def build_program(NSEQ, S, layers, do_final, dbg=False):
    nc = bass.Bass("TRN2", target_bir_lowering=False)
    NTILES = S // T

    def din(name, shape, dt=F32):
        return nc.dram_tensor(name, shape, dt, kind="ExternalInput").ap()

    x_d = din("x", [NSEQ, S, D])
    vecs_d = din("vecs", [NVROW, 128])
    tabs_d = din("tabs", [128, NTAB])
    rope_d = din("rope", [128, 2, S])
    ada_w_d = din("ada_w", [DEPTH, D, 9 * D])
    fwi_d = din("ffn_w_in", [DEPTH, 2, D, 2 * DFF])
    fwo_d = din("ffn_w_out", [DEPTH, 2, DFF, D])
    rwi_d = din("ret_w_in", [2, D, 6 * D])
    rwo_d = din("ret_w_out", [2, 2 * D, D])
    cwi_d = din("conv_w_in", [2, D, 3 * D])
    cwo_d = din("conv_w_out", [2, D, D])
    y_d = nc.dram_tensor("y", [NSEQ, S, D], F32, kind="ExternalOutput").ap()

    def dscr(name, shape):
        return nc.dram_tensor(name, shape, BF16, kind="Internal").ap()

    s_fin = dscr("s_fin", [DEPTH, 2, 11, 128, 4096])
    s_fout = dscr("s_fout", [DEPTH, 2, 8, 128, 2816])
    s_rin = dscr("s_rin", [2, 12, 128, 4096])
    s_rout = dscr("s_rout", [2, 4, 128, 4096])
    s_cin = dscr("s_cin", [2, 6, 128, 4096])
    s_cout = dscr("s_cout", [2, 2, 128, 4096])

    with ExitStack() as st:
        E = st.enter_context

        def sb(name, shape, dt):
            return E(nc.sbuf_tensor(name, shape, dt))

        WORKB = 106496
        work_t = sb("work", [128, WORKB // 4], F32)
        xT_t = sb("xT", [128, KC, T], F32)
        S_t = sb("Sst", [128, 2, NH * 2 * DV], BF16)
        ws_t = sb("wslots", [128, NSLOT, SLOT_ELEMS], BF16)
        mask_t = sb("maskb", [128, NH, T], BF16)
        qdec_t = sb("qdecb", [128, NH, T], BF16)
        rope_t = sb("ropew", [128, 2, T], F32)
        VT_t = sb("VT", [128, NVROW], F32)
        MOD_t = sb("MOD", [128, DEPTH, 72, 2], F32)
        AA_t = sb("AA", [128, DEPTH * 3, KC, 2], F32)
        GG_t = sb("GG", [128, DEPTH * 3, KC, 2], F32)
        GN_t = sb("GN", [128, 32], F32)
        FG_t = sb("FG", [128, KC], F32)
        kdec_t = sb("kdec", [128, 16], F32)
        id32_t = sb("id32", [128, 128], F32)
        idb_t = sb("idb", [128, 128], BF16)
        ones_t = sb("onesb", [128, 128], BF16)
        carry_t = sb("carry", [128, 2, KC, 2], F32)
        cond_t = sb("cond", [128, KC, 2], F32)
        condb_t = sb("condb", [128, KC, 2], BF16)
        lnd_t = sb("lnd", [128, 2], F32)

        ps_t = [E(nc.psum_tensor(f"ps{i}", [128, 512], F32)) for i in range(8)]

        def sem(name):
            return E(nc.semaphore(name))

        PE = Eng(nc.tensor, sem("s_pe"))
        ACT = Eng(nc.scalar, sem("s_act"))
        DVE = Eng(nc.vector, sem("s_dve"))
        POOL = Eng(nc.gpsimd, sem("s_pool"))
        SP = Eng(nc.sync, sem("s_sp"))

        dcnt = {}

        def dma(q, dsem, out, in_, reads=(), writes=()):
            q.wait(_gather(reads, writes))
            q.h.dma_start(out=out, in_=in_).then_inc(dsem, 16)
            dcnt[dsem] = dcnt.get(dsem, 0) + 16
            ev = (dsem, dcnt[dsem])
            _commit(ev, reads, writes)
            return ev

        W = Region(work_t[:])
        banks = [Buf(ps_t[i][:]) for i in range(8)]
        bank_i = [0]

        def bank():
            b = banks[bank_i[0] % 8]
            bank_i[0] += 1
            return b

        def mmgroup(ps, items, extra_reads=()):
            PE.wait(_gather(extra_reads, [ps]))
            allr = list(extra_reads)
            last = None
            for out_ap, lhsT, rhs, st_, sp_, rb in items:
                PE.wait(_gather(rb, []))
                last = nc.tensor.matmul(out_ap, lhsT=lhsT, rhs=rhs, start=st_, stop=sp_)
                allr += rb
            ev = PE.sig(last)
            _commit(ev, allr, [ps])
            return ev

        xTB = [Buf(xT_t[:, kc, :]) for kc in range(KC)]
        SB_ = [[Buf(S_t[:, j, hh * 1024:(hh + 1) * 1024].rearrange("p (a b) -> p a b", a=2))
                for hh in range(NH)] for j in range(2)]
        slotB = [Buf(ws_t[:, k, :]) for k in range(NSLOT)]
        slot_sem = [sem(f"s_w{k}") for k in range(NSLOT)]
        ropeB = Buf(rope_t[:])
        carryB = [Buf(carry_t[:, j]) for j in range(2)]
        constB = Buf(None)
        sem_misc = sem("s_misc")
        sem_misc2 = sem("s_misc2")
        sem_xio = sem("s_xio")
        XIO_OFF = 53248
        XIN_OFF = 90112
        xin_next = [None]
        sem_rope = sem("s_rope")
        sem_st = [sem("s_st0"), sem("s_st1"), sem("s_st2")]
        sem_ld = [sem("s_ld0"), sem("s_ld1"), sem("s_ld2")]
        sem_out = sem("s_out")
        sem_mld = sem("s_mld0")

        rr = [0]

        def evac_eng():
            rr[0] += 1
            return ACT if rr[0] % 2 else DVE

        tabsB = W.buf(0, [NTAB], F32)
        vrawB = W.buf(NTAB * 4, [4, 128], F32)
        dma(SP, sem_misc, tabsB.ap, tabs_d, writes=[tabsB])
        dma(SP, sem_misc2, vrawB.ap, vecs_d.rearrange("(b p) e -> p b e", p=128), writes=[vrawB])
        tabs = tabsB.ap
        cw = []
        cw.append(op(DVE, lambda: nc.vector.tensor_copy(out=mask_t[:].rearrange("p a b -> p (a b)"), in_=tabs[:, 0:2048]), reads=[tabsB]))
        cw.append(op(DVE, lambda: nc.vector.tensor_copy(out=qdec_t[:].rearrange("p a b -> p (a b)"), in_=tabs[:, 2048:4096]), reads=[tabsB]))
        cw.append(op(DVE, lambda: nc.vector.tensor_copy(out=kdec_t[:], in_=tabs[:, 4096:4112]), reads=[tabsB]))
        cw.append(op(DVE, lambda: nc.vector.tensor_copy(out=id32_t[:], in_=tabs[:, 4112:4240]), reads=[tabsB]))
        cw.append(op(DVE, lambda: nc.vector.tensor_copy(out=idb_t[:], in_=tabs[:, 4112:4240]), reads=[tabsB]))
        cw.append(op(DVE, lambda: nc.vector.memset(ones_t[:], 1.0)))
        cw.append(op(DVE, lambda: nc.vector.memset(lnd_t[:], 1.0)))
        for ev in cw:
            _mg(constB.w, ev[0], ev[1])
        pb = bank()
        op(PE, lambda: [nc.tensor.transpose(out=pb.ap[:, b * 128:(b + 1) * 128], in_=vrawB.ap[:, b, :], identity=id32_t[:])
                        for b in range(4)][-1], reads=[vrawB, constB], writes=[pb])
        VTB = Buf(VT_t[:])
        op(DVE, lambda: nc.vector.tensor_copy(out=VT_t[:], in_=pb.ap), reads=[pb], writes=[VTB])
        VT = VT_t
        condB = Buf(cond_t[:])
        op(ACT, lambda: nc.scalar.activation(out=cond_t[:].rearrange("p k b -> p b k"),
                                             in_=VT[:, R_C:R_C + 16].rearrange("p (b k) -> p b k", b=2), func=AF.Silu),
           reads=[VTB], writes=[condB])
        modB = Buf(None)
        ev = op(DVE, lambda: nc.vector.tensor_scalar(out=GN_t[:], in0=VT[:, R_GN:R_GN + 32], scalar1=float(np.sqrt(DV)), scalar2=None, op0=ALU.mult), reads=[VTB])
        _mg(modB.w, ev[0], ev[1])
        ev = op(DVE, lambda: nc.vector.tensor_scalar(out=FG_t[:], in0=VT[:, R_FG:R_FG + 8], scalar1=float(np.sqrt(D)), scalar2=None, op0=ALU.mult), reads=[VTB])
        _mg(modB.w, ev[0], ev[1])

        ws_flat = ws_t[:].rearrange("p a b -> p (a b)")
        mst32 = Buf(ws_flat[:, 0:8192].bitcast(F32).rearrange("p (a b) -> p a b", a=KC))
        mstb = [Buf(ws_flat[:, 8192 + i * 4096:8192 + (i + 1) * 4096].rearrange("p (a b) -> p a b", a=KC)) for i in range(2)]
        condbB = Buf(condb_t[:])
        op(DVE, lambda: nc.vector.tensor_copy(out=condb_t[:], in_=cond_t[:]), reads=[condB], writes=[condbB])
        nblk = [0]
        pm_cur = [None]

        def mod_step(l, blk):
            if blk == 0:
                pm_cur[0] = bank()
            pm = pm_cur[0]
            dma(SP, sem_mld, mst32.ap, ada_w_d[l].rearrange("(kc p) n -> p kc n", p=128)[:, :, blk * 512:(blk + 1) * 512], writes=[mst32])
            sb16 = mstb[nblk[0] % 2]
            if nblk[0] % 2 == 0:
                op(ACT, lambda: nc.scalar.activation(out=sb16.ap, in_=mst32.ap, func=AF.Copy), reads=[mst32], writes=[sb16])
            else:
                op(DVE, lambda: nc.vector.tensor_copy(out=sb16.ap, in_=mst32.ap), reads=[mst32], writes=[sb16])
            nblk[0] += 1

            def mm_mod():
                last = None
                for c4 in range(4):
                    cc = blk * 4 + c4
                    for kc in range(KC):
                        last = nc.tensor.matmul(pm.ap[:, cc * 2:cc * 2 + 2], lhsT=sb16.ap[:, kc, c4 * 128:(c4 + 1) * 128],
                                                rhs=condb_t[:, kc, :], start=(kc == 0), stop=(kc == KC - 1))
                return last
            if blk == 0:
                op(PE, mm_mod, reads=[sb16, condbB], writes=[pm])
            else:
                PE.wait(_gather([sb16, condbB], []))
                ev = PE.sig(mm_mod())
                _commit(ev, [sb16], [])
                pm.w = {ev[0]: ev[1]}
            if blk == 17:
                ev = op(DVE, lambda: nc.vector.tensor_tensor(
                    out=MOD_t[:, l], in0=pm.ap[:, 0:144].rearrange("p (c b) -> p c b", b=2),
                    in1=VT[:, R_AB + l * 72:R_AB + (l + 1) * 72].unsqueeze(2).broadcast_to([128, 72, 2]), op=ALU.add),
                    reads=[pm, VTB])
                _mg(modB.w, ev[0], ev[1])
                for s in range(3):
                    gscale = 0.5 if s != 1 else 1.0
                    r0 = R_NG + (l * 3 + s) * 8
                    ev = op(DVE, lambda: nc.vector.scalar_tensor_tensor(
                        out=AA_t[:, l * 3 + s], in0=MOD_t[:, l, s * 24 + 8:s * 24 + 16, :], scalar=1.0,
                        in1=VT[:, r0:r0 + 8].unsqueeze(2).broadcast_to([128, KC, 2]), op0=ALU.add, op1=ALU.mult),
                        reads=[modB, VTB])
                    _mg(modB.w, ev[0], ev[1])
                    ev = op(DVE, lambda: nc.vector.tensor_scalar(
                        out=AA_t[:, l * 3 + s], in0=AA_t[:, l * 3 + s], scalar1=float(np.sqrt(D)), scalar2=None, op0=ALU.mult),
                        reads=[modB])
                    _mg(modB.w, ev[0], ev[1])
                    ev = op(DVE, lambda: nc.vector.tensor_scalar(
                        out=GG_t[:, l * 3 + s], in0=MOD_t[:, l, s * 24 + 16:s * 24 + 24, :], scalar1=gscale, scalar2=None, op0=ALU.mult),
                        reads=[modB])
                    _mg(modB.w, ev[0], ev[1])

        mod_steps = [(l, blk) for l in layers for blk in range(18)]
        for _ in range(4):
            if mod_steps:
                mod_step(*mod_steps.pop(0))

        W.reset()
        ST32 = 22528
        ST16 = 11264
        NST = 3
        st32 = [W.buf(i * (ST32 + ST16), [ST32 // 4], F32) for i in range(NST)]
        st16 = [W.buf(i * (ST32 + ST16) + ST32, [ST16 // 2], BF16) for i in range(NST)]
        cast_engs = [DVE, ACT]
        njob = [0]
        ncast = [0]

        pend_st = []

        def flush_stores():
            while pend_st:
                segs_, b16_, i_ = pend_st.pop(0)
                for iv, n, scr, ov, o in segs_:
                    dma(SP, sem_st[i_], scr, b16_.ap[:, o:o + n], reads=[b16_])

        def cast_job(loads, outs):
            i = njob[0] % NST
            njob[0] += 1
            b32, b16 = st32[i], st16[i]
            SP.wait(_gather([], [b32]))
            for dv, src in loads:
                SP.h.dma_start(out=dv(b32.ap), in_=src).then_inc(sem_ld[i], 16)
                dcnt[sem_ld[i]] = dcnt.get(sem_ld[i], 0) + 16
            _commit((sem_ld[i], dcnt[sem_ld[i]]), [], [b32])
            flush_stores()
            off = 0
            segs = []
            for iv, n, scr, ov in outs:
                segs.append((iv, n, scr, ov, off))
                off += n

            eng = cast_engs[ncast[0] % len(cast_engs)]
            ncast[0] += 1

            def do_cast():
                last = None
                for iv, n, scr, ov, o in segs:
                    dst = ov(b16.ap[:, o:o + n])
                    src_ap = iv(b32.ap)
                    if eng is ACT:
                        last = nc.scalar.activation(out=dst, in_=src_ap, func=AF.Copy)
                    elif eng is DVE:
                        last = nc.vector.tensor_copy(out=dst, in_=src_ap)
                    else:
                        last = nc.gpsimd.tensor_copy(out=dst, in_=src_ap)
                return last
            op(eng, do_cast, reads=[b32], writes=[b16])
            pend_st.append((segs, b16, i))
            if njob[0] % 2 == 0 and mod_steps:
                mod_step(*mod_steps.pop(0))

        def v3(a, b):
            return lambda ap: ap[:, 0:a * b].rearrange("p (a b) -> p a b", a=a)

        def cast_std(src3, scr, kc=KC, n=512):
            cast_job([(v3(kc, n), src3)], [(v3(kc, n), kc * n, scr, v3(kc, n))])

        for l in layers:
            j = l // 2
            for i in range(2):
                wi = fwi_d[l, i].rearrange("(kc p) (gu n) -> p kc gu n", p=128, gu=2)
                for g in range(11):
                    def dv(gu):
                        return lambda ap: ap[:, 0:4096].rearrange("p (k u n) -> p k u n", k=KC, u=2)[:, :, gu, :]
                    full = lambda ap: ap[:, 0:4096]
                    cast_job([(dv(0), wi[:, :, 0, g * 256:(g + 1) * 256]), (dv(1), wi[:, :, 1, g * 256:(g + 1) * 256])],
                             [(full, 4096, s_fin[l, i, g], full)])
                wo = fwo_d[l, i].rearrange("(f p) n -> p f n", p=128)
                for g in range(4):
                    def iv(mi):
                        return lambda ap: ap[:, 0:NF * 256].rearrange("p (f n) -> p f n", f=NF)[:, :, mi * 128:(mi + 1) * 128]
                    cast_job([(v3(NF, 256), wo[:, :, g * 256:(g + 1) * 256])],
                             [(iv(0), 2816, s_fout[l, i, 2 * g], v3(NF, 128)),
                              (iv(1), 2816, s_fout[l, i, 2 * g + 1], v3(NF, 128))])
            if l % 2 == 0:
                wi = rwi_d[j].rearrange("(kc p) n -> p kc n", p=128)
                for blk in range(12):
                    cast_std(wi[:, :, blk * 512:(blk + 1) * 512], s_rin[j, blk])
                wo = rwo_d[j].rearrange("(c p) n -> p c n", p=128)
                for g in range(4):
                    cast_std(wo[:, :, g * 256:(g + 1) * 256], s_rout[j, g], kc=16, n=256)
            else:
                wi = cwi_d[j].rearrange("(kc p) n -> p kc n", p=128)
                for blk in range(6):
                    cast_std(wi[:, :, blk * 512:(blk + 1) * 512], s_cin[j, blk])
                wo = cwo_d[j].rearrange("(kc p) n -> p kc n", p=128)
                for blk in range(2):
                    cast_std(wo[:, :, blk * 512:(blk + 1) * 512], s_cout[j, blk])
        flush_stores()
        while mod_steps:
            mod_step(*mod_steps.pop(0))
        for k in range(NSLOT):
            for mb in [mst32] + mstb:
                for q, v in list(mb.w.items()) + list(mb.r.items()):
                    _mg(slotB[k].r, q, v)
        SP.wait({q: dcnt.get(q, 0) for q in sem_st})

        wi_cnt = [0]

        def wnext(scr_ap, n):
            k = wi_cnt[0] % NSLOT
            wi_cnt[0] += 1
            b = slotB[k]
            dma(SP, slot_sem[k], ws_t[:, k, 0:n], scr_ap, writes=[b])
            return b, ws_t[:, k, :]

        def preload_ln():
            op(ACT, lambda: nc.scalar.activation(out=lnd_t[:, 1:2], in_=lnd_t[:, 0:1], func=AF.Ln), reads=[constB])

        def modulate(l, s, b):
            sq = [W.buf(kc * 1024, [T], BF16) for kc in range(KC)]
            tmp = [W.buf(8192 + kc * 2048, [T], F32) for kc in range(KC)]
            r1 = W.buf(24576, [T], F32)
            rs = W.buf(26624, [T], F32)
            h = [W.buf(28672 + kc * 1024, [T], BF16) for kc in range(KC)]
            preload_ln()
            for kc in range(KC):
                op(ACT, lambda: nc.scalar.activation(out=sq[kc].ap, in_=xT_t[:, kc, :], func=AF.Square),
                   reads=[xTB[kc]], writes=[sq[kc]])
            pst = bank()
            mmgroup(pst, [(pst.ap, ones_t[:], sq[kc].ap, kc == 0, kc == KC - 1, [sq[kc]]) for kc in range(KC)], [constB])
            op(ACT, lambda: nc.scalar.activation(out=r1.ap, in_=pst.ap, func=AF.Ln, bias=float(D * EPS), scale=1.0),
               reads=[pst], writes=[r1])
            op(ACT, lambda: nc.scalar.activation(out=rs.ap, in_=r1.ap, func=AF.Exp, scale=-0.5), reads=[r1], writes=[rs])
            for kc in range(KC):
                op(DVE, lambda: nc.vector.tensor_tensor(out=tmp[kc].ap, in0=xT_t[:, kc, :], in1=rs.ap, op=ALU.mult),
                   reads=[xTB[kc], rs], writes=[tmp[kc]])
            for kc in range(KC):
                op(ACT, lambda: nc.scalar.activation(out=h[kc].ap, in_=tmp[kc].ap, func=AF.Identity,
                                                     bias=MOD_t[:, l, s * 24 + kc, b:b + 1], scale=AA_t[:, l * 3 + s, kc, b:b + 1]),
                   reads=[tmp[kc], modB], writes=[h[kc]])
            return h

        def resid(l, s, b, m, ps):
            op(DVE, lambda: nc.vector.scalar_tensor_tensor(out=xT_t[:, m, :], in0=ps.ap, scalar=GG_t[:, l * 3 + s, m, b:b + 1],
                                                           in1=xT_t[:, m, :], op0=ALU.mult, op1=ALU.add),
               reads=[ps, modB], writes=[xTB[m]])

        def ffn(l, i, b, prefetch=None):
            s = 0 if i == 0 else 2
            W.reset()
            if prefetch is not None:
                nb_ = W.buf(XIN_OFF, [4, D], F32)
                ps_, pt_ = prefetch
                dma(SP, sem_xio, nb_.ap, x_d[ps_, pt_ * T:(pt_ + 1) * T, :].rearrange("(tb p) d -> p tb d", p=128), writes=[nb_])
                xin_next[0] = nb_
            h = modulate(l, s, b)
            act = [W.buf(36864 + f * 1024, [T], BF16) for f in range(NF)]
            sg = [W.buf(36864 + NF * 1024 + q * 2048, [T], F32) for q in range(2)]
            for g in range(11):
                wb, wap = wnext(s_fin[l, i, g], 4096)
                wv = wap.rearrange("p (k u f n) -> p k u f n", k=KC, u=2, f=2)
                for fi in range(2):
                    f = 2 * g + fi
                    pg = bank()
                    mmgroup(pg, [(pg.ap, wv[:, kc, 0, fi, :], h[kc].ap, kc == 0, kc == KC - 1, [h[kc]]) for kc in range(KC)], [wb])
                    pu = bank()
                    mmgroup(pu, [(pu.ap, wv[:, kc, 1, fi, :], h[kc].ap, kc == 0, kc == KC - 1, [h[kc]]) for kc in range(KC)], [wb])
                    sgb = sg[f % 2]
                    op(ACT, lambda: nc.scalar.activation(out=sgb.ap, in_=pg.ap, func=AF.Silu), reads=[pg], writes=[sgb])
                    op(DVE, lambda: nc.vector.tensor_tensor(out=act[f].ap, in0=pu.ap, in1=sgb.ap, op=ALU.mult),
                       reads=[pu, sgb], writes=[act[f]])
            for m in range(KC):
                wb, wap = wnext(s_fout[l, i, m], 2816)
                wv = wap[:, 0:2816].rearrange("p (f n) -> p f n", f=NF)
                py = bank()
                mmgroup(py, [(py.ap, wv[:, f, :], act[f].ap, f == 0, f == NF - 1, [act[f]]) for f in range(NF)], [wb])
                resid(l, s, b, m, py)

        def retention(l, b, first):
            j = l // 2
            W.reset()
            h = modulate(l, 1, b)
            R2 = 36864
            qT = [W.buf(R2 + c * 1024, [T], BF16) for c in range(8)]
            kT = [W.buf(R2 + 8192 + c * 1024, [T], BF16) for c in range(8)]
            qd = [W.buf(R2 + 16384 + hh * 2048, [2, T], BF16) for hh in range(NH)]
            ktok = [W.buf(R2 + 24576 + tb * 2048, [NH, DK], BF16) for tb in range(4)]
            vtok = [W.buf(R2 + 32768 + tb * 4096, [NH * DV], BF16) for tb in range(4)]
            gs = [W.buf(R2 + 49152 + c * 1024, [T], BF16) for c in range(16)]
            W.sync()
            a1 = [W.buf(q * 2048, [T], F32) for q in range(2)]
            a2 = [W.buf(4096 + q * 2048, [T], F32) for q in range(2)]
            tt = [[W.buf(8192 + (q * 4 + r) * 2048, [T], F32) for r in range(4)] for q in range(2)]
            cosv = rope_t[:, 0, :]
            sinv = rope_t[:, 1, :]
            pair_i = 0
            for blk in (2, 3, 0, 1):
                wb, wap = wnext(s_rin[j, blk], 4096)
                wv = wap.rearrange("p (k n) -> p k n", k=KC)
                isk = blk >= 2
                for pr in range(2):
                    hh = (blk % 2) * 2 + pr
                    dstl = kT if isk else qT
                    p1 = bank()
                    mmgroup(p1, [(p1.ap, wv[:, kc, pr * 256:pr * 256 + 128], h[kc].ap, kc == 0, kc == KC - 1, [h[kc]]) for kc in range(KC)], [wb])
                    p2 = bank()
                    mmgroup(p2, [(p2.ap, wv[:, kc, pr * 256 + 128:pr * 256 + 256], h[kc].ap, kc == 0, kc == KC - 1, [h[kc]]) for kc in range(KC)], [wb])
                    A1 = a1[pair_i % 2]
                    A2 = a2[pair_i % 2]
                    t1, t2, t3, t4 = tt[pair_i % 2]
                    pair_i += 1
                    sc = (DK ** -0.5) if isk else 1.0
                    op(ACT, lambda: nc.scalar.activation(out=A1.ap, in_=p1.ap, func=AF.Copy, scale=sc), reads=[p1], writes=[A1])
                    op(ACT, lambda: nc.scalar.activation(out=A2.ap, in_=p2.ap, func=AF.Copy, scale=sc), reads=[p2], writes=[A2])
                    c1, c2 = dstl[hh * 2], dstl[hh * 2 + 1]
                    op(POOL, lambda: nc.gpsimd.tensor_tensor(out=t2.ap, in0=A2.ap, in1=sinv, op=ALU.mult), reads=[A2, ropeB], writes=[t2])
                    op(DVE, lambda: nc.vector.tensor_tensor(out=t1.ap, in0=A1.ap, in1=cosv, op=ALU.mult), reads=[A1, ropeB], writes=[t1])
                    op(POOL, lambda: nc.gpsimd.tensor_tensor(out=t3.ap, in0=A2.ap, in1=cosv, op=ALU.mult), reads=[A2, ropeB], writes=[t3])
                    op(DVE, lambda: nc.vector.tensor_tensor(out=t4.ap, in0=A1.ap, in1=sinv, op=ALU.mult), reads=[A1, ropeB], writes=[t4])
                    op(DVE, lambda: nc.vector.tensor_tensor(out=c1.ap, in0=t1.ap, in1=t2.ap, op=ALU.subtract), reads=[t1, t2], writes=[c1])
                    op(DVE, lambda: nc.vector.tensor_tensor(out=c2.ap, in0=t3.ap, in1=t4.ap, op=ALU.add), reads=[t3, t4], writes=[c2])
                    if not isk:
                        op(DVE, lambda: nc.vector.tensor_tensor(
                            out=qd[hh].ap, in0=W.ap(R2 + hh * 2048, [2, T], BF16),
                            in1=qdec_t[:, hh, :].unsqueeze(1).broadcast_to([128, 2, T]), op=ALU.mult),
                            reads=[c1, c2, constB], writes=[qd[hh]])
            for tb in range(4):
                pk = bank()
                pkb = pk.ap.bitcast(BF16)
                op(PE, lambda: [nc.tensor.transpose(out=pkb[:, c * 128:(c + 1) * 128], in_=kT[c].ap[:, tb * 128:(tb + 1) * 128], identity=idb_t[:])
                                for c in range(8)][-1], reads=kT + [constB], writes=[pk])
                op(DVE, lambda: nc.vector.tensor_tensor(
                    out=ktok[tb].ap, in0=pkb.rearrange("p (a b) -> p a b", a=NH),
                    in1=kdec_t[:].rearrange("p (a b) -> p a b", a=NH)[:, :, tb:tb + 1].broadcast_to([128, NH, DK]), op=ALU.mult),
                    reads=[pk, constB], writes=[ktok[tb]])
            for nb in range(4):
                wb, wap = wnext(s_rin[j, 4 + nb], 4096)
                wv = wap.rearrange("p (k n) -> p k n", k=KC)
                for tb in range(4):
                    pv = bank()
                    op(PE, lambda: [nc.tensor.matmul(pv.ap, lhsT=h[kc].ap[:, tb * 128:(tb + 1) * 128], rhs=wv[:, kc, :], start=(kc == 0), stop=(kc == KC - 1))
                                    for kc in range(KC)][-1], reads=h + [wb], writes=[pv])
                    if nb == 0:
                        op(ACT, lambda: nc.scalar.activation(out=vtok[tb].ap[:, nb * 512:(nb + 1) * 512], in_=pv.ap, func=AF.Copy),
                           reads=[pv], writes=[vtok[tb]])
                    else:
                        ACT.wait(_gather([pv], []))
                        ev = ACT.sig(nc.scalar.activation(out=vtok[tb].ap[:, nb * 512:(nb + 1) * 512], in_=pv.ap, func=AF.Copy))
                        _commit(ev, [pv], [])
                        _mg(vtok[tb].w, ev[0], ev[1])
            for gb in range(4):
                wb, wap = wnext(s_rin[j, 8 + gb], 4096)
                wv = wap.rearrange("p (k n) -> p k n", k=KC)
                for c4 in range(4):
                    c = gb * 4 + c4
                    pg = bank()
                    op(PE, lambda: [nc.tensor.matmul(pg.ap, lhsT=wv[:, kc, c4 * 128:(c4 + 1) * 128], rhs=h[kc].ap, start=(kc == 0), stop=(kc == KC - 1))
                                    for kc in range(KC)][-1], reads=h + [wb], writes=[pg])
                    op(ACT, lambda: nc.scalar.activation(out=gs[c].ap, in_=pg.ap, func=AF.Silu), reads=[pg], writes=[gs[c]])
            preload_ln()
            W.sync()
            Pb = [[W.buf(q * 4096 + jb * 1024, [T], BF16) for jb in range(4)] for q in range(2)]
            o32_ = [[W.buf(8192 + q * 8192 + vc * 2048, [T], F32) for vc in range(4)] for q in range(2)]
            sqo_ = [[W.buf(24576 + q * 4096 + vc * 1024, [T], BF16) for vc in range(4)] for q in range(2)]
            r1 = W.buf(32768, [T], F32)
            rs = W.buf(34816, [T], F32)
            to = [W.buf(102400 + q * 2048, [T], F32) for q in range(2)]
            Sst = SB_[j]

            def scores(hh):
                P_ = Pb[hh % 2]
                for jb in range(4):
                    n = (4 - jb) * 128
                    pss = bank()
                    op(PE, lambda: [nc.tensor.matmul(pss.ap[:, 0:n], lhsT=kT[hh * 2 + dc].ap[:, jb * 128:(jb + 1) * 128],
                                                     rhs=qT[hh * 2 + dc].ap[:, jb * 128:T], start=(dc == 0), stop=(dc == 1))
                                    for dc in range(2)][-1], reads=[kT[hh * 2], kT[hh * 2 + 1], qT[hh * 2], qT[hh * 2 + 1]], writes=[pss])
                    op(DVE, lambda: nc.vector.tensor_tensor(out=P_[jb].ap[:, 0:n], in0=pss.ap[:, 0:n], in1=mask_t[:, hh, 0:n], op=ALU.mult),
                       reads=[pss, constB], writes=[P_[jb]])

            scores(0)
            for hh in range(NH):
                P_ = Pb[hh % 2]
                o32 = o32_[hh % 2]
                sqo = sqo_[hh % 2]
                for vc in range(4):
                    po = bank()

                    def mm_o():
                        last = None
                        if not first:
                            for dc in range(2):
                                last = nc.tensor.matmul(po.ap, lhsT=Sst[hh].ap[:, dc, vc * 128:(vc + 1) * 128], rhs=qd[hh].ap[:, dc, :],
                                                        start=(dc == 0), stop=False)
                        for jb in range(4):
                            n = (4 - jb) * 128
                            last = nc.tensor.matmul(po.ap[:, jb * 128:T], lhsT=vtok[jb].ap[:, hh * DV + vc * 128:hh * DV + (vc + 1) * 128],
                                                    rhs=P_[jb].ap[:, 0:n], start=(first and jb == 0), stop=(jb == 3))
                        return last
                    op(PE, mm_o, reads=vtok + P_ + ([] if first else [Sst[hh], qd[hh]]), writes=[po])
                    op(ACT, lambda: nc.scalar.activation(out=o32[vc].ap, in_=po.ap, func=AF.Copy), reads=[po], writes=[o32[vc]])
                    op(ACT, lambda: nc.scalar.activation(out=sqo[vc].ap, in_=po.ap, func=AF.Square), reads=[po], writes=[sqo[vc]])
                    if vc == 1 and hh + 1 < NH:
                        scores(hh + 1)
                pst = bank()
                op(PE, lambda: [nc.tensor.matmul(pst.ap, lhsT=ones_t[:], rhs=sqo[vc].ap, start=(vc == 0), stop=(vc == 3))
                                for vc in range(4)][-1], reads=sqo + [constB], writes=[pst])
                op(ACT, lambda: nc.scalar.activation(out=r1.ap, in_=pst.ap, func=AF.Ln, bias=float(DV * EPS), scale=1.0),
                   reads=[pst], writes=[r1])
                op(ACT, lambda: nc.scalar.activation(out=rs.ap, in_=r1.ap, func=AF.Exp, scale=-0.5), reads=[r1], writes=[rs])
                for vc in range(4):
                    c = hh * 4 + vc
                    tob = to[vc % 2]
                    op(DVE, lambda: nc.vector.scalar_tensor_tensor(out=tob.ap, in0=o32[vc].ap, scalar=GN_t[:, j * 16 + c:j * 16 + c + 1],
                                                                   in1=rs.ap, op0=ALU.mult, op1=ALU.mult),
                       reads=[o32[vc], rs, modB], writes=[tob])
                    op(POOL, lambda: nc.gpsimd.tensor_tensor(out=gs[c].ap, in0=tob.ap, in1=gs[c].ap, op=ALU.mult),
                       reads=[tob], writes=[gs[c]])
            for hh in range(NH):
                for dc in range(2):
                    pd = bank()
                    op(PE, lambda: [nc.tensor.matmul(pd.ap, lhsT=ktok[jb].ap[:, hh, dc * 128:(dc + 1) * 128],
                                                     rhs=vtok[jb].ap[:, hh * DV:(hh + 1) * DV], start=(jb == 0), stop=(jb == 3))
                                    for jb in range(4)][-1], reads=ktok + vtok, writes=[pd])
                    if dc == 0:
                        if first:
                            op(ACT, lambda: nc.scalar.activation(out=Sst[hh].ap[:, dc, :], in_=pd.ap, func=AF.Copy), reads=[pd], writes=[Sst[hh]])
                        else:
                            op(DVE, lambda: nc.vector.scalar_tensor_tensor(out=Sst[hh].ap[:, dc, :], in0=Sst[hh].ap[:, dc, :],
                                                                           scalar=float(GAMMA[hh] ** T), in1=pd.ap, op0=ALU.mult, op1=ALU.add),
                               reads=[pd], writes=[Sst[hh]])
                    else:
                        e_ = ACT if first else DVE
                        e_.wait(_gather([pd], []))
                        if first:
                            inst = nc.scalar.activation(out=Sst[hh].ap[:, dc, :], in_=pd.ap, func=AF.Copy)
                        else:
                            inst = nc.vector.scalar_tensor_tensor(out=Sst[hh].ap[:, dc, :], in0=Sst[hh].ap[:, dc, :],
                                                                  scalar=float(GAMMA[hh] ** T), in1=pd.ap, op0=ALU.mult, op1=ALU.add)
                        ev = e_.sig(inst)
                        _commit(ev, [pd], [])
                        _mg(Sst[hh].w, ev[0], ev[1])
            for g in range(4):
                wb, wap = wnext(s_rout[j, g], 4096)
                wv = wap.rearrange("p (c n) -> p c n", c=16)
                for mi in range(2):
                    m = 2 * g + mi
                    py = bank()
                    mmgroup(py, [(py.ap, wv[:, c, mi * 128:(mi + 1) * 128], gs[c].ap, c == 0, c == 15, [gs[c]]) for c in range(16)], [wb])
                    resid(l, 1, b, m, py)

        def shortconv(l, b, first):
            j = l // 2
            W.reset()
            h = modulate(l, 1, b)
            R2 = 36864
            tcv = [W.buf(R2 + c * 2048, [T], F32) for c in range(8)]
            csb = [W.buf(R2 + 16384 + c * 2048, [T], F32) for c in range(8)]
            zb = [W.buf(R2 + 32768 + c * 2064, [T + 4], F32) for c in range(8)]
            W.sync()
            gc = [W.buf(c * 1024, [T], BF16) for c in range(8)]
            cwr = R_CW + j * 24
            for blk in (2, 3, 4, 5, 0, 1):
                wb, wap = wnext(s_cin[j, blk], 4096)
                wv = wap.rearrange("p (k n) -> p k n", k=KC)
                kind = blk // 2
                for c4 in range(4):
                    c = (blk % 2) * 4 + c4
                    pp = bank()
                    mmgroup(pp, [(pp.ap, wv[:, kc, c4 * 128:(c4 + 1) * 128], h[kc].ap, kc == 0, kc == KC - 1, [h[kc]]) for kc in range(KC)], [wb])
                    if kind == 1:
                        op(ACT, lambda: nc.scalar.activation(out=csb[c].ap, in_=pp.ap, func=AF.Copy), reads=[pp], writes=[csb[c]])
                    elif kind == 2:
                        z = zb[c]
                        t = tcv[c]
                        if first:
                            op(POOL, lambda: nc.gpsimd.memset(z.ap[:, 0:2], 0.0), writes=[z])
                        else:
                            op(POOL, lambda: nc.gpsimd.tensor_copy(out=z.ap[:, 0:2], in_=carry_t[:, j, c, :]), reads=[carryB[j]], writes=[z])
                        DVE.wait(_gather([pp, csb[c]], [z]))
                        ev = DVE.sig(nc.vector.tensor_tensor(out=z.ap[:, 2:T + 2], in0=pp.ap, in1=csb[c].ap, op=ALU.mult))
                        _commit(ev, [pp, csb[c]], [z])
                        op(ACT, lambda: nc.scalar.activation(out=t.ap, in_=z.ap[:, 2:T + 2], func=AF.Identity, bias=0.0,
                                                             scale=VT_t[:, cwr + 16 + c:cwr + 16 + c + 1]), reads=[z, VTB], writes=[t])
                        op(DVE, lambda: nc.vector.scalar_tensor_tensor(out=t.ap, in0=z.ap[:, 1:T + 1], scalar=VT_t[:, cwr + 8 + c:cwr + 8 + c + 1],
                                                                       in1=t.ap, op0=ALU.mult, op1=ALU.add), reads=[z, VTB], writes=[t])
                        op(DVE, lambda: nc.vector.scalar_tensor_tensor(out=t.ap, in0=z.ap[:, 0:T], scalar=VT_t[:, cwr + c:cwr + c + 1],
                                                                       in1=t.ap, op0=ALU.mult, op1=ALU.add), reads=[z, VTB], writes=[t])
                        if c == 0:
                            op(POOL, lambda: nc.gpsimd.tensor_copy(out=carry_t[:, j, c, :], in_=z.ap[:, T:T + 2]), reads=[z], writes=[carryB[j]])
                        else:
                            POOL.wait(_gather([z], []))
                            ev = POOL.sig(nc.gpsimd.tensor_copy(out=carry_t[:, j, c, :], in_=z.ap[:, T:T + 2]))
                            _commit(ev, [z], [])
                            _mg(carryB[j].w, ev[0], ev[1])
                    else:
                        op(DVE, lambda: nc.vector.tensor_tensor(out=gc[c].ap, in0=pp.ap, in1=tcv[c].ap, op=ALU.mult),
                           reads=[pp, tcv[c]], writes=[gc[c]])
            for blk in range(2):
                wb, wap = wnext(s_cout[j, blk], 4096)
                wv = wap.rearrange("p (k n) -> p k n", k=KC)
                for m4 in range(4):
                    m = blk * 4 + m4
                    py = bank()
                    mmgroup(py, [(py.ap, wv[:, c, m4 * 128:(m4 + 1) * 128], gc[c].ap, c == 0, c == KC - 1, [gc[c]]) for c in range(KC)], [wb])
                    resid(l, 1, b, m, py)

        def final_norm():
            W.reset()
            sq = [W.buf(kc * 1024, [T], BF16) for kc in range(KC)]
            r1 = W.buf(24576, [T], F32)
            rs = W.buf(26624, [T], F32)
            yT = [W.buf(36864 + kc * 2048, [T], F32) for kc in range(KC)]
            preload_ln()
            for kc in range(KC):
                op(ACT, lambda: nc.scalar.activation(out=sq[kc].ap, in_=xT_t[:, kc, :], func=AF.Square), reads=[xTB[kc]], writes=[sq[kc]])
            pst = bank()
            op(PE, lambda: [nc.tensor.matmul(pst.ap, lhsT=ones_t[:], rhs=sq[kc].ap, start=(kc == 0), stop=(kc == KC - 1))
                            for kc in range(KC)][-1], reads=sq + [constB], writes=[pst])
            op(ACT, lambda: nc.scalar.activation(out=r1.ap, in_=pst.ap, func=AF.Ln, bias=float(D * EPS), scale=1.0), reads=[pst], writes=[r1])
            op(ACT, lambda: nc.scalar.activation(out=rs.ap, in_=r1.ap, func=AF.Exp, scale=-0.5), reads=[r1], writes=[rs])
            for kc in range(KC):
                op(DVE, lambda: nc.vector.scalar_tensor_tensor(out=yT[kc].ap, in0=xT_t[:, kc, :], scalar=FG_t[:, kc:kc + 1], in1=rs.ap,
                                                               op0=ALU.mult, op1=ALU.mult), reads=[xTB[kc], rs, modB], writes=[yT[kc]])
            return yT

        for seq in range(NSEQ):
            for ti in range(NTILES):
                first = (ti == 0)
                t0 = ti * T
                dma(SP, sem_rope, rope_t[:], rope_d[:, :, t0:t0 + T], writes=[ropeB])
                W.reset()
                if xin_next[0] is None:
                    xioB = W.buf(XIN_OFF, [4, D], F32)
                    dma(SP, sem_xio, xioB.ap, x_d[seq, t0:t0 + T, :].rearrange("(tb p) d -> p tb d", p=128), writes=[xioB])
                else:
                    xioB = xin_next[0]
                    xin_next[0] = None
                    W.bufs.append(xioB)
                for kc in range(KC):
                    pb = bank()
                    op(PE, lambda: [nc.tensor.transpose(out=pb.ap[:, tb * 128:(tb + 1) * 128], in_=xioB.ap[:, tb, kc * 128:(kc + 1) * 128], identity=id32_t[:])
                                    for tb in range(4)][-1], reads=[xioB, constB], writes=[pb])
                    if kc % 2 == 0:
                        op(ACT, lambda: nc.scalar.activation(out=xT_t[:, kc, :], in_=pb.ap, func=AF.Copy), reads=[pb], writes=[xTB[kc]])
                    else:
                        op(DVE, lambda: nc.vector.tensor_copy(out=xT_t[:, kc, :], in_=pb.ap), reads=[pb], writes=[xTB[kc]])
                nxt = (seq, ti + 1) if ti + 1 < NTILES else ((seq + 1, 0) if seq + 1 < NSEQ else None)
                for l in layers:
                    ffn(l, 0, seq)
                    if l % 2 == 0:
                        retention(l, seq, first)
                    else:
                        shortconv(l, seq, first)
                    ffn(l, 1, seq, prefetch=(nxt if (l == layers[-1] and X_PREFETCH) else None))
                if do_final:
                    yT = final_norm()
                    ysrc = [(yT[kc].ap, yT[kc]) for kc in range(KC)]
                else:
                    W.reset()
                    ysrc = [(xT_t[:, kc, :], xTB[kc]) for kc in range(KC)]
                xioB = W.buf(XIO_OFF, [4, D], F32)
                xio_v = xioB.ap
                for tb in range(4):
                    for hf in range(2):
                        pb = bank()
                        op(PE, lambda: [nc.tensor.transpose(out=pb.ap[:, k4 * 128:(k4 + 1) * 128], in_=ysrc[hf * 4 + k4][0][:, tb * 128:(tb + 1) * 128],
                                                            identity=id32_t[:]) for k4 in range(4)][-1],
                           reads=[ysrc[hf * 4 + k4][1] for k4 in range(4)] + [constB], writes=[pb])
                        if tb == 0 and hf == 0:
                            op(ACT, lambda: nc.scalar.activation(out=xio_v[:, tb, hf * 512:(hf + 1) * 512], in_=pb.ap, func=AF.Copy), reads=[pb], writes=[xioB])
                        else:
                            e_ = ACT if (tb * 2 + hf) % 2 == 0 else DVE
                            e_.wait(_gather([pb], []))
                            if e_ is ACT:
                                inst = nc.scalar.activation(out=xio_v[:, tb, hf * 512:(hf + 1) * 512], in_=pb.ap, func=AF.Copy)
                            else:
                                inst = nc.vector.tensor_copy(out=xio_v[:, tb, hf * 512:(hf + 1) * 512], in_=pb.ap)
                            ev = e_.sig(inst)
                            _commit(ev, [pb], [])
                            _mg(xioB.w, ev[0], ev[1])
                dma(ACT if OUT_ON_ACT else SP, sem_out, y_d[seq, t0:t0 + T, :].rearrange("(tb p) d -> p tb d", p=128), xio_v, reads=[xioB])
        SP.wait({sem_out: dcnt[sem_out]})
        if dbg:
            print("instr counts:", {k: v.cnt for k, v in dict(PE=PE, ACT=ACT, DVE=DVE, POOL=POOL).items()},
                  "waits:", {k: v.nwait for k, v in dict(PE=PE, ACT=ACT, DVE=DVE, POOL=POOL, SP=SP).items()})
    return nc


def _const_tables(S):
    half = DK // 2
    inv_freq = (10000.0 ** (-np.arange(half, dtype=np.float64) * 2.0 / DK)).astype(np.float32)
    pos = np.arange(S, dtype=np.float32)
    ang = (inv_freq[:, None] * pos[None, :]).astype(np.float32)
    rope = np.stack([np.cos(ang.astype(np.float64)), np.sin(ang.astype(np.float64))], axis=1).astype(np.float32)
    tabs = np.zeros((128, NTAB), np.float32)
    jj = np.arange(128, dtype=np.float64)[:, None]
    ii = np.arange(T, dtype=np.float64)[None, :]
    for hh in range(NH):
        g = GAMMA[hh]
        dlt = ii - jj
        m = np.where(dlt >= 0, g ** np.maximum(dlt, 0.0), 0.0)
        tabs[:, hh * T:(hh + 1) * T] = m
        tabs[:, 2048 + hh * T:2048 + (hh + 1) * T] = (g ** (ii + 1.0))
        for jb in range(4):
            tabs[:, 4096 + hh * 4 + jb] = g ** (T - 1.0 - (jb * 128 + jj[:, 0]))
    tabs[:, 4112:4240] = np.eye(128, dtype=np.float32)
    return np.ascontiguousarray(rope), tabs


def _pack_vecs(c2, ada_b, norm_g, final_g, conv_w, ret_gn_g):
    v = np.zeros((NVROW, 128), np.float32)
    cr = c2.reshape(-1, 128)
    v[R_C:R_C + cr.shape[0]] = cr
    v[R_NG:R_NG + 96] = norm_g.reshape(96, 128)
    v[R_FG:R_FG + 8] = final_g.reshape(8, 128)
    v[R_CW:R_CW + 48] = conv_w.reshape(48, 128)
    v[R_GN:R_GN + 32] = ret_gn_g.reshape(32, 128)
    v[R_AB:R_AB + 288] = ada_b.reshape(288, 128)
    return v


_prog_cache = {}


def _get_prog(NSEQ, S, layers, do_final):
    key = (NSEQ, S, tuple(layers), do_final)
    if key not in _prog_cache:
        _prog_cache[key] = build_program(NSEQ, S, list(layers), do_final)
    return _prog_cache[key]


def run_layers(x, c, ada_w, ada_b, norm_g, ffn_w_in, ffn_w_out, ret_w_in, ret_gn_g, ret_w_out,
               conv_w_in, conv_w, conv_w_out, final_g, layers, do_final, ncores):
    B, S, _ = x.shape
    nseq = B // ncores
    rope, tabs = _const_tables(S)
    f = lambda a: np.ascontiguousarray(np.asarray(a, dtype=np.float32))
    shared = dict(tabs=tabs, rope=rope, ada_w=f(ada_w), ffn_w_in=f(ffn_w_in), ffn_w_out=f(ffn_w_out),
                  ret_w_in=f(ret_w_in), ret_w_out=f(ret_w_out), conv_w_in=f(conv_w_in), conv_w_out=f(conv_w_out))
    x = f(x)
    c = f(c)
    in_maps = []
    for i in range(ncores):
        m = dict(shared)
        m["x"] = np.ascontiguousarray(x[i * nseq:(i + 1) * nseq])
        m["vecs"] = _pack_vecs(c[i * nseq:(i + 1) * nseq], f(ada_b), f(norm_g), f(final_g), f(conv_w), f(ret_gn_g))
        in_maps.append(m)
    nc = _get_prog(nseq, S, layers, do_final)
    res = run_bass_kernel_spmd(nc, in_maps, core_ids=list(range(ncores)))
    return np.concatenate([r["y"] for r in res.results], axis=0)


def kernel(x, c, ada_w, ada_b, norm_g, ffn_w_in, ffn_w_out, ret_w_in, ret_gn_g, ret_w_out,
           conv_w_in, conv_w, conv_w_out, final_g):
    return run_layers(x, c, ada_w, ada_b, norm_g, ffn_w_in, ffn_w_out, ret_w_in, ret_gn_g, ret_w_out,
                      conv_w_in, conv_w, conv_w_out, final_g, layers=(0, 1, 2, 3), do_final=True, ncores=NCORES)
```

```python
import os
import numpy as np
from contextlib import ExitStack

import concourse.bass as bass
import concourse.mybir as mybir
from concourse.bass_utils import run_bass_kernel_spmd

F32 = mybir.dt.float32
BF16 = mybir.dt.bfloat16
F32R = mybir.dt.float32r
AF = mybir.ActivationFunctionType
ALU = mybir.AluOpType

D = 1024
KC = 8
DFF = 2816
NF = 22
NH = 4
DK = 256
DV = 512
T = 512
DEPTH = 4
EPS = 1e-6
SEQ = 4096
NCORES = 8
GAMMA = [1.0 - 2.0 ** (-5.0 - h) for h in range(NH)]
NSLOT = 6
SLOT_ELEMS = 4096
NTAB = 2048 + 2048 + 16 + 128
OUT_ON_ACT = os.environ.get('K_OUT_ACT', '1') == '1'
X_PREFETCH = os.environ.get('K_PREF', '1') == '1'

R_C = 0
R_NG = 16
R_FG = 112
R_CW = 120
R_GN = 168
R_AB = 200
NVROW = 512


class Eng:
    def __init__(self, h, sem):
        self.h = h
        self.sem = sem
        self.cnt = 0
        self.seen = {}
        self.nwait = 0

    def wait(self, evs):
        for sem, val in evs.items():
            if self.seen.get(sem, 0) < val:
                self.h.wait_ge(sem, val)
                self.seen[sem] = val
                self.nwait += 1

    def sig(self, inst):
        self.cnt += 1
        inst.then_inc(self.sem, 1)
        return (self.sem, self.cnt)


class Buf:
    __slots__ = ("ap", "w", "r")

    def __init__(self, ap, pend=None):
        self.ap = ap
        self.w = {}
        self.r = dict(pend) if pend else {}


def _mg(d, s, v):
    if d.get(s, 0) < v:
        d[s] = v


def _gather(reads, writes):
    evs = {}
    for b in reads:
        for s, v in b.w.items():
            _mg(evs, s, v)
    for b in writes:
        for s, v in b.w.items():
            _mg(evs, s, v)
        for s, v in b.r.items():
            _mg(evs, s, v)
    return evs


def _commit(ev, reads, writes):
    for b in reads:
        _mg(b.r, ev[0], ev[1])
    for b in writes:
        b.w = {ev[0]: ev[1]}
        b.r = {}


def op(eng, fn, reads=(), writes=()):
    eng.wait(_gather(reads, writes))
    inst = fn()
    ev = eng.sig(inst)
    _commit(ev, reads, writes)
    return ev


class Region:
    def __init__(self, base_f32):
        self.base = base_f32
        self.bufs = []
        self.pend = {}

    def sync(self):
        for b in self.bufs:
            for s, v in b.w.items():
                _mg(self.pend, s, v)
            for s, v in b.r.items():
                _mg(self.pend, s, v)

    def reset(self):
        self.sync()
        self.bufs = []

    def ap(self, off, shape, dt):
        n = int(np.prod(shape))
        nbytes = n * (4 if dt == F32 else 2)
        assert off % 4 == 0 and nbytes % 4 == 0
        a = self.base[:, off // 4:(off + nbytes) // 4]
        if dt == BF16:
            a = a.bitcast(BF16)
        if len(shape) == 2:
            a = a.rearrange("p (a b) -> p a b", a=shape[0])
        elif len(shape) == 3:
            a = a.rearrange("p (a b c) -> p a b c", a=shape[0], b=shape[1])
        return a

    def buf(self, off, shape, dt):
        b = Buf(self.ap(off, shape, dt), self.pend)
        self.bufs.append(b)
        return b


def build_program(NSEQ, S, layers, do_final, dbg=False):
    nc = bass.Bass("TRN2", target_bir_lowering=False)
    NTILES = S // T

    def din(name, shape, dt=F32):
        return nc.dram_tensor(name, shape, dt, kind="ExternalInput").ap()

    x_d = din("x", [NSEQ, S, D])
    vecs_d = din("vecs", [NVROW, 128])
    tabs_d = din("tabs", [128, NTAB])
    rope_d = din("rope", [128, 2, S])
    ada_w_d = din("ada_w", [DEPTH, D, 9 * D])
    fwi_d = din("ffn_w_in", [DEPTH, 2, D, 2 * DFF])
    fwo_d = din("ffn_w_out", [DEPTH, 2, DFF, D])
    rwi_d = din("ret_w_in", [2, D, 6 * D])
    rwo_d = din("ret_w_out", [2, 2 * D, D])
    cwi_d = din("conv_w_in", [2, D, 3 * D])
    cwo_d = din("conv_w_out", [2, D, D])
    y_d = nc.dram_tensor("y", [NSEQ, S, D], F32, kind="ExternalOutput").ap()

    def dscr(name, shape):
        return nc.dram_tensor(name, shape, BF16, kind="Internal").ap()

    s_fin = dscr("s_fin", [DEPTH, 2, 11, 128, 4096])
    s_fout = dscr("s_fout", [DEPTH, 2, 8, 128, 2816])
    s_rin = dscr("s_rin", [2, 12, 128, 4096])
    s_rout = dscr("s_rout", [2, 4, 128, 4096])
    s_cin = dscr("s_cin", [2, 6, 128, 4096])
    s_cout = dscr("s_cout", [2, 2, 128, 4096])

    with ExitStack() as st:
        E = st.enter_context

        def sb(name, shape, dt):
            return E(nc.sbuf_tensor(name, shape, dt))

        WORKB = 106496
        work_t = sb("work", [128, WORKB // 4], F32)
        xT_t = sb("xT", [128, KC, T], F32)
        S_t = sb("Sst", [128, 2, NH * 2 * DV], BF16)
        ws_t = sb("wslots", [128, NSLOT, SLOT_ELEMS], BF16)
        mask_t = sb("maskb", [128, NH, T], BF16)
        qdec_t = sb("qdecb", [128, NH, T], BF16)
        rope_t = sb("ropew", [128, 2, T], F32)
        VT_t = sb("VT", [128, NVROW], F32)
        MOD_t = sb("MOD", [128, DEPTH, 72, 2], F32)
        AA_t = sb("AA", [128, DEPTH * 3, KC, 2], F32)
        GG_t = sb("GG", [128, DEPTH * 3, KC, 2], F32)
        GN_t = sb("GN", [128, 32], F32)
        FG_t = sb("FG", [128, KC], F32)
        kdec_t = sb("kdec", [128, 16], F32)
        id32_t = sb("id32", [128, 128], F32)
        idb_t = sb("idb", [128, 128], BF16)
        ones_t = sb("onesb", [128, 128], BF16)
        carry_t = sb("carry", [128, 2, KC, 2], F32)
        cond_t = sb("cond", [128, KC, 2], F32)
        condb_t = sb("condb", [128, KC, 2], BF16)
        lnd_t = sb("lnd", [128, 2], F32)

        ps_t = [E(nc.psum_tensor(f"ps{i}", [128, 512], F32)) for i in range(8)]

        def sem(name):
            return E(nc.semaphore(name))

        PE = Eng(nc.tensor, sem("s_pe"))
        ACT = Eng(nc.scalar, sem("s_act"))
        DVE = Eng(nc.vector, sem("s_dve"))
        POOL = Eng(nc.gpsimd, sem("s_pool"))
        SP = Eng(nc.sync, sem("s_sp"))

        dcnt = {}

        def dma(q, dsem, out, in_, reads=(), writes=()):
            q.wait(_gather(reads, writes))
            q.h.dma_start(out=out, in_=in_).then_inc(dsem, 16)
            dcnt[dsem] = dcnt.get(dsem, 0) + 16
            ev = (dsem, dcnt[dsem])
            _commit(ev, reads, writes)
            return ev

        W = Region(work_t[:])
        banks = [Buf(ps_t[i][:]) for i in range(8)]
        bank_i = [0]

        def bank():
            b = banks[bank_i[0] % 8]
            bank_i[0] += 1
            return b

        def mmgroup(ps, items, extra_reads=()):
            PE.wait(_gather(extra_reads, [ps]))
            allr = list(extra_reads)
            last = None
            for out_ap, lhsT, rhs, st_, sp_, rb in items:
                PE.wait(_gather(rb, []))
                last = nc.tensor.matmul(out_ap, lhsT=lhsT, rhs=rhs, start=st_, stop=sp_)
                allr += rb
            ev = PE.sig(last)
            _commit(ev, allr, [ps])
            return ev

        xTB = [Buf(xT_t[:, kc, :]) for kc in range(KC)]
        SB_ = [[Buf(S_t[:, j, hh * 1024:(hh + 1) * 1024].rearrange("p (a b) -> p a b", a=2))
                for hh in range(NH)] for j in range(2)]
        slotB = [Buf(ws_t[:, k, :]) for k in range(NSLOT)]
        slot_sem = [sem(f"s_w{k}") for k in range(NSLOT)]
        ropeB = Buf(rope_t[:])
        carryB = [Buf(carry_t[:, j]) for j in range(2)]
        constB = Buf(None)
        sem_misc = sem("s_misc")
        sem_misc2 = sem("s_misc2")
        sem_xio = sem("s_xio")
        XIO_OFF = 53248
        XIN_OFF = 90112
        xin_next = [None]
        sem_rope = sem("s_rope")
        sem_st = [sem("s_st0"), sem("s_st1"), sem("s_st2")]
        sem_ld = [sem("s_ld0"), sem("s_ld1"), sem("s_ld2")]
        sem_out = sem("s_out")
        sem_mld = [sem("s_mld0"), sem("s_mld1")]

        rr = [0]

        def evac_eng():
            rr[0] += 1
            return ACT if rr[0] % 2 else DVE

        tabsB = W.buf(0, [NTAB], F32)
        vrawB = W.buf(NTAB * 4, [4, 128], F32)
        dma(SP, sem_misc, tabsB.ap, tabs_d, writes=[tabsB])
        dma(SP, sem_misc2, vrawB.ap, vecs_d.rearrange("(b p) e -> p b e", p=128), writes=[vrawB])
        tabs = tabsB.ap
        cw = []
        cw.append(op(DVE, lambda: nc.vector.tensor_copy(out=mask_t[:].rearrange("p a b -> p (a b)"), in_=tabs[:, 0:2048]), reads=[tabsB]))
        cw.append(op(DVE, lambda: nc.vector.tensor_copy(out=qdec_t[:].rearrange("p a b -> p (a b)"), in_=tabs[:, 2048:4096]), reads=[tabsB]))
        cw.append(op(DVE, lambda: nc.vector.tensor_copy(out=kdec_t[:], in_=tabs[:, 4096:4112]), reads=[tabsB]))
        cw.append(op(DVE, lambda: nc.vector.tensor_copy(out=id32_t[:], in_=tabs[:, 4112:4240]), reads=[tabsB]))
        cw.append(op(DVE, lambda: nc.vector.tensor_copy(out=idb_t[:], in_=tabs[:, 4112:4240]), reads=[tabsB]))
        cw.append(op(DVE, lambda: nc.vector.memset(ones_t[:], 1.0)))
        cw.append(op(DVE, lambda: nc.vector.memset(lnd_t[:], 1.0)))
        for ev in cw:
            _mg(constB.w, ev[0], ev[1])
        pb = bank()
        op(PE, lambda: [nc.tensor.transpose(out=pb.ap[:, b * 128:(b + 1) * 128], in_=vrawB.ap[:, b, :], identity=id32_t[:])
                        for b in range(4)][-1], reads=[vrawB, constB], writes=[pb])
        VTB = Buf(VT_t[:])
        op(DVE, lambda: nc.vector.tensor_copy(out=VT_t[:], in_=pb.ap), reads=[pb], writes=[VTB])
        VT = VT_t
        condB = Buf(cond_t[:])
        op(ACT, lambda: nc.scalar.activation(out=cond_t[:].rearrange("p k b -> p b k"),
                                             in_=VT[:, R_C:R_C + 16].rearrange("p (b k) -> p b k", b=2), func=AF.Silu),
           reads=[VTB], writes=[condB])
        modB = Buf(None)
        ev = op(DVE, lambda: nc.vector.tensor_scalar(out=GN_t[:], in0=VT[:, R_GN:R_GN + 32], scalar1=float(np.sqrt(DV)), scalar2=None, op0=ALU.mult), reads=[VTB])
        _mg(modB.w, ev[0], ev[1])
        ev = op(DVE, lambda: nc.vector.tensor_scalar(out=FG_t[:], in0=VT[:, R_FG:R_FG + 8], scalar1=float(np.sqrt(D)), scalar2=None, op0=ALU.mult), reads=[VTB])
        _mg(modB.w, ev[0], ev[1])

        W.reset()
        mst = [W.buf(i * 16384, [KC, 512], F32) for i in range(2)]
        mstb = [W.buf(32768 + i * 8192, [KC, 512], BF16) for i in range(2)]
        condbB = Buf(condb_t[:])
        op(DVE, lambda: nc.vector.tensor_copy(out=condb_t[:], in_=cond_t[:]), reads=[condB], writes=[condbB])
        nblk = 0
        for l in layers:
            pm = bank()
            for blk in range(18):
                sbuf = mst[nblk % 2]
                dma(SP, sem_ld[nblk % 2], sbuf.ap,
                    ada_w_d[l].rearrange("(kc p) n -> p kc n", p=128)[:, :, blk * 512:(blk + 1) * 512], writes=[sbuf])
                sb16 = mstb[nblk % 2]
                if nblk % 2 == 0:
                    op(ACT, lambda: nc.scalar.activation(out=sb16.ap, in_=sbuf.ap, func=AF.Copy), reads=[sbuf], writes=[sb16])
                else:
                    op(DVE, lambda: nc.vector.tensor_copy(out=sb16.ap, in_=sbuf.ap), reads=[sbuf], writes=[sb16])
                nblk += 1

                def mm_mod(sb16=sb16, blk=blk, pm=pm):
                    last = None
                    for c4 in range(4):
                        cc = blk * 4 + c4
                        for kc in range(KC):
                            last = nc.tensor.matmul(pm.ap[:, cc * 2:cc * 2 + 2], lhsT=sb16.ap[:, kc, c4 * 128:(c4 + 1) * 128],
                                                    rhs=condb_t[:, kc, :], start=(kc == 0), stop=(kc == KC - 1))
                    return last
                if blk == 0:
                    op(PE, mm_mod, reads=[sb16, condbB], writes=[pm])
                else:
                    PE.wait(_gather([sb16, condbB], []))
                    inst = mm_mod()
                    ev = PE.sig(inst)
                    _commit(ev, [sb16], [])
                    pm.w = {ev[0]: ev[1]}
            ev = op(DVE, lambda: nc.vector.tensor_tensor(
                out=MOD_t[:, l], in0=pm.ap[:, 0:144].rearrange("p (c b) -> p c b", b=2),
                in1=VT[:, R_AB + l * 72:R_AB + (l + 1) * 72].unsqueeze(2).broadcast_to([128, 72, 2]), op=ALU.add),
                reads=[pm, VTB])
            _mg(modB.w, ev[0], ev[1])
            for s in range(3):
                gscale = 0.5 if s != 1 else 1.0
                r0 = R_NG + (l * 3 + s) * 8
                ev = op(DVE, lambda: nc.vector.scalar_tensor_tensor(
                    out=AA_t[:, l * 3 + s], in0=MOD_t[:, l, s * 24 + 8:s * 24 + 16, :], scalar=1.0,
                    in1=VT[:, r0:r0 + 8].unsqueeze(2).broadcast_to([128, KC, 2]), op0=ALU.add, op1=ALU.mult),
                    reads=[modB, VTB])
                _mg(modB.w, ev[0], ev[1])
                ev = op(DVE, lambda: nc.vector.tensor_scalar(
                    out=AA_t[:, l * 3 + s], in0=AA_t[:, l * 3 + s], scalar1=float(np.sqrt(D)), scalar2=None, op0=ALU.mult),
                    reads=[modB])
                _mg(modB.w, ev[0], ev[1])
                ev = op(DVE, lambda: nc.vector.tensor_scalar(
                    out=GG_t[:, l * 3 + s], in0=MOD_t[:, l, s * 24 + 16:s * 24 + 24, :], scalar1=gscale, scalar2=None, op0=ALU.mult),
                    reads=[modB])
                _mg(modB.w, ev[0], ev[1])

        W.reset()
        ST32 = 22528
        ST16 = 11264
        NST = 3
        st32 = [W.buf(i * (ST32 + ST16), [ST32 // 4], F32) for i in range(NST)]
        st16 = [W.buf(i * (ST32 + ST16) + ST32, [ST16 // 2], BF16) for i in range(NST)]
        cast_engs = [DVE, ACT]
        njob = [0]
        ncast = [0]

        pend_st = []

        def flush_stores():
            while pend_st:
                segs_, b16_, i_ = pend_st.pop(0)
                for iv, n, scr, ov, o in segs_:
                    dma(SP, sem_st[i_], scr, b16_.ap[:, o:o + n], reads=[b16_])

        def cast_job(loads, outs):
            i = njob[0] % NST
            njob[0] += 1
            b32, b16 = st32[i], st16[i]
            SP.wait(_gather([], [b32]))
            for dv, src in loads:
                SP.h.dma_start(out=dv(b32.ap), in_=src).then_inc(sem_ld[i], 16)
                dcnt[sem_ld[i]] = dcnt.get(sem_ld[i], 0) + 16
            _commit((sem_ld[i], dcnt[sem_ld[i]]), [], [b32])
            flush_stores()
            off = 0
            segs = []
            for iv, n, scr, ov in outs:
                segs.append((iv, n, scr, ov, off))
                off += n

            eng = cast_engs[ncast[0] % len(cast_engs)]
            ncast[0] += 1

            def do_cast():
                last = None
                for iv, n, scr, ov, o in segs:
                    dst = ov(b16.ap[:, o:o + n])
                    src_ap = iv(b32.ap)
                    if eng is ACT:
                        last = nc.scalar.activation(out=dst, in_=src_ap, func=AF.Copy)
                    elif eng is DVE:
                        last = nc.vector.tensor_copy(out=dst, in_=src_ap)
                    else:
                        last = nc.gpsimd.tensor_copy(out=dst, in_=src_ap)
                return last
            op(eng, do_cast, reads=[b32], writes=[b16])
            pend_st.append((segs, b16, i))

        def v3(a, b):
            return lambda ap: ap[:, 0:a * b].rearrange("p (a b) -> p a b", a=a)

        def cast_std(src3, scr, kc=KC, n=512):
            cast_job([(v3(kc, n), src3)], [(v3(kc, n), kc * n, scr, v3(kc, n))])

        for l in layers:
            j = l // 2
            for i in range(2):
                wi = fwi_d[l, i].rearrange("(kc p) (gu n) -> p kc gu n", p=128, gu=2)
                for g in range(11):
                    def dv(gu):
                        return lambda ap: ap[:, 0:4096].rearrange("p (k u n) -> p k u n", k=KC, u=2)[:, :, gu, :]
                    full = lambda ap: ap[:, 0:4096]
                    cast_job([(dv(0), wi[:, :, 0, g * 256:(g + 1) * 256]), (dv(1), wi[:, :, 1, g * 256:(g + 1) * 256])],
                             [(full, 4096, s_fin[l, i, g], full)])
                wo = fwo_d[l, i].rearrange("(f p) n -> p f n", p=128)
                for g in range(4):
                    def iv(mi):
                        return lambda ap: ap[:, 0:NF * 256].rearrange("p (f n) -> p f n", f=NF)[:, :, mi * 128:(mi + 1) * 128]
                    cast_job([(v3(NF, 256), wo[:, :, g * 256:(g + 1) * 256])],
                             [(iv(0), 2816, s_fout[l, i, 2 * g], v3(NF, 128)),
                              (iv(1), 2816, s_fout[l, i, 2 * g + 1], v3(NF, 128))])
            if l % 2 == 0:
                wi = rwi_d[j].rearrange("(kc p) n -> p kc n", p=128)
                for blk in range(12):
                    cast_std(wi[:, :, blk * 512:(blk + 1) * 512], s_rin[j, blk])
                wo = rwo_d[j].rearrange("(c p) n -> p c n", p=128)
                for g in range(4):
                    cast_std(wo[:, :, g * 256:(g + 1) * 256], s_rout[j, g], kc=16, n=256)
            else:
                wi = cwi_d[j].rearrange("(kc p) n -> p kc n", p=128)
                for blk in range(6):
                    cast_std(wi[:, :, blk * 512:(blk + 1) * 512], s_cin[j, blk])
                wo = cwo_d[j].rearrange("(kc p) n -> p kc n", p=128)
                for blk in range(2):
                    cast_std(wo[:, :, blk * 512:(blk + 1) * 512], s_cout[j, blk])
        flush_stores()
        SP.wait({q: dcnt.get(q, 0) for q in sem_st})

        wi_cnt = [0]

        def wnext(scr_ap, n):
            k = wi_cnt[0] % NSLOT
            wi_cnt[0] += 1
            b = slotB[k]
            dma(SP, slot_sem[k], ws_t[:, k, 0:n], scr_ap, writes=[b])
            return b, ws_t[:, k, :]

        def preload_ln():
            op(ACT, lambda: nc.scalar.activation(out=lnd_t[:, 1:2], in_=lnd_t[:, 0:1], func=AF.Ln), reads=[constB])

        def modulate(l, s, b):
            sq = [W.buf(kc * 1024, [T], BF16) for kc in range(KC)]
            tmp = [W.buf(8192 + kc * 2048, [T], F32) for kc in range(KC)]
            r1 = W.buf(24576, [T], F32)
            rs = W.buf(26624, [T], F32)
            h = [W.buf(28672 + kc * 1024, [T], BF16) for kc in range(KC)]
            preload_ln()
            for kc in range(KC):
                op(ACT, lambda: nc.scalar.activation(out=sq[kc].ap, in_=xT_t[:, kc, :], func=AF.Square),
                   reads=[xTB[kc]], writes=[sq[kc]])
            pst = bank()
            mmgroup(pst, [(pst.ap, ones_t[:], sq[kc].ap, kc == 0, kc == KC - 1, [sq[kc]]) for kc in range(KC)], [constB])
            op(ACT, lambda: nc.scalar.activation(out=r1.ap, in_=pst.ap, func=AF.Ln, bias=float(D * EPS), scale=1.0),
               reads=[pst], writes=[r1])
            op(ACT, lambda: nc.scalar.activation(out=rs.ap, in_=r1.ap, func=AF.Exp, scale=-0.5), reads=[r1], writes=[rs])
            for kc in range(KC):
                op(DVE, lambda: nc.vector.tensor_tensor(out=tmp[kc].ap, in0=xT_t[:, kc, :], in1=rs.ap, op=ALU.mult),
                   reads=[xTB[kc], rs], writes=[tmp[kc]])
            for kc in range(KC):
                op(ACT, lambda: nc.scalar.activation(out=h[kc].ap, in_=tmp[kc].ap, func=AF.Identity,
                                                     bias=MOD_t[:, l, s * 24 + kc, b:b + 1], scale=AA_t[:, l * 3 + s, kc, b:b + 1]),
                   reads=[tmp[kc], modB], writes=[h[kc]])
            return h

        def resid(l, s, b, m, ps):
            op(DVE, lambda: nc.vector.scalar_tensor_tensor(out=xT_t[:, m, :], in0=ps.ap, scalar=GG_t[:, l * 3 + s, m, b:b + 1],
                                                           in1=xT_t[:, m, :], op0=ALU.mult, op1=ALU.add),
               reads=[ps, modB], writes=[xTB[m]])

        def ffn(l, i, b, prefetch=None):
            s = 0 if i == 0 else 2
            W.reset()
            nb_ = W.buf(XIN_OFF, [4, D], F32) if prefetch is not None else None
            h = modulate(l, s, b)
            act = [W.buf(36864 + f * 1024, [T], BF16) for f in range(NF)]
            sg = [W.buf(36864 + NF * 1024 + q * 2048, [T], F32) for q in range(2)]
            for g in range(11):
                if g == 6 and nb_ is not None:
                    ps_, pt_ = prefetch
                    dma(SP, sem_xio, nb_.ap, x_d[ps_, pt_ * T:(pt_ + 1) * T, :].rearrange("(tb p) d -> p tb d", p=128), writes=[nb_])
                    xin_next[0] = nb_
                wb, wap = wnext(s_fin[l, i, g], 4096)
                wv = wap.rearrange("p (k u f n) -> p k u f n", k=KC, u=2, f=2)
                for fi in range(2):
                    f = 2 * g + fi
                    pg = bank()
                    mmgroup(pg, [(pg.ap, wv[:, kc, 0, fi, :], h[kc].ap, kc == 0, kc == KC - 1, [h[kc]]) for kc in range(KC)], [wb])
                    pu = bank()
                    mmgroup(pu, [(pu.ap, wv[:, kc, 1, fi, :], h[kc].ap, kc == 0, kc == KC - 1, [h[kc]]) for kc in range(KC)], [wb])
                    sgb = sg[f % 2]
                    op(ACT, lambda: nc.scalar.activation(out=sgb.ap, in_=pg.ap, func=AF.Silu), reads=[pg], writes=[sgb])
                    op(DVE, lambda: nc.vector.tensor_tensor(out=act[f].ap, in0=pu.ap, in1=sgb.ap, op=ALU.mult),
                       reads=[pu, sgb], writes=[act[f]])
            for m in range(KC):
                wb, wap = wnext(s_fout[l, i, m], 2816)
                wv = wap[:, 0:2816].rearrange("p (f n) -> p f n", f=NF)
                py = bank()
                mmgroup(py, [(py.ap, wv[:, f, :], act[f].ap, f == 0, f == NF - 1, [act[f]]) for f in range(NF)], [wb])
                resid(l, s, b, m, py)

        def retention(l, b, first):
            j = l // 2
            W.reset()
            h = modulate(l, 1, b)
            R2 = 36864
            qT = [W.buf(R2 + c * 1024, [T], BF16) for c in range(8)]
            kT = [W.buf(R2 + 8192 + c * 1024, [T], BF16) for c in range(8)]
            qd = [W.buf(R2 + 16384 + hh * 2048, [2, T], BF16) for hh in range(NH)]
            ktok = [W.buf(R2 + 24576 + tb * 2048, [NH, DK], BF16) for tb in range(4)]
            vtok = [W.buf(R2 + 32768 + tb * 4096, [NH * DV], BF16) for tb in range(4)]
            gs = [W.buf(R2 + 49152 + c * 1024, [T], BF16) for c in range(16)]
            W.sync()
            a1 = [W.buf(q * 2048, [T], F32) for q in range(2)]
            a2 = [W.buf(4096 + q * 2048, [T], F32) for q in range(2)]
            tt = [[W.buf(8192 + (q * 4 + r) * 2048, [T], F32) for r in range(4)] for q in range(2)]
            cosv = rope_t[:, 0, :]
            sinv = rope_t[:, 1, :]
            pair_i = 0
            for blk in (2, 3, 0, 1):
                wb, wap = wnext(s_rin[j, blk], 4096)
                wv = wap.rearrange("p (k n) -> p k n", k=KC)
                isk = blk >= 2
                for pr in range(2):
                    hh = (blk % 2) * 2 + pr
                    dstl = kT if isk else qT
                    p1 = bank()
                    mmgroup(p1, [(p1.ap, wv[:, kc, pr * 256:pr * 256 + 128], h[kc].ap, kc == 0, kc == KC - 1, [h[kc]]) for kc in range(KC)], [wb])
                    p2 = bank()
                    mmgroup(p2, [(p2.ap, wv[:, kc, pr * 256 + 128:pr * 256 + 256], h[kc].ap, kc == 0, kc == KC - 1, [h[kc]]) for kc in range(KC)], [wb])
                    A1 = a1[pair_i % 2]
                    A2 = a2[pair_i % 2]
                    t1, t2, t3, t4 = tt[pair_i % 2]
                    pair_i += 1
                    sc = (DK ** -0.5) if isk else 1.0
                    op(ACT, lambda: nc.scalar.activation(out=A1.ap, in_=p1.ap, func=AF.Copy, scale=sc), reads=[p1], writes=[A1])
                    op(ACT, lambda: nc.scalar.activation(out=A2.ap, in_=p2.ap, func=AF.Copy, scale=sc), reads=[p2], writes=[A2])
                    c1, c2 = dstl[hh * 2], dstl[hh * 2 + 1]
                    op(POOL, lambda: nc.gpsimd.tensor_tensor(out=t2.ap, in0=A2.ap, in1=sinv, op=ALU.mult), reads=[A2, ropeB], writes=[t2])
                    op(DVE, lambda: nc.vector.tensor_tensor(out=t1.ap, in0=A1.ap, in1=cosv, op=ALU.mult), reads=[A1, ropeB], writes=[t1])
                    op(POOL, lambda: nc.gpsimd.tensor_tensor(out=t3.ap, in0=A2.ap, in1=cosv, op=ALU.mult), reads=[A2, ropeB], writes=[t3])
                    op(DVE, lambda: nc.vector.tensor_tensor(out=t4.ap, in0=A1.ap, in1=sinv, op=ALU.mult), reads=[A1, ropeB], writes=[t4])
                    op(DVE, lambda: nc.vector.tensor_tensor(out=c1.ap, in0=t1.ap, in1=t2.ap, op=ALU.subtract), reads=[t1, t2], writes=[c1])
                    op(DVE, lambda: nc.vector.tensor_tensor(out=c2.ap, in0=t3.ap, in1=t4.ap, op=ALU.add), reads=[t3, t4], writes=[c2])
                    if not isk:
                        op(DVE, lambda: nc.vector.tensor_tensor(
                            out=qd[hh].ap, in0=W.ap(R2 + hh * 2048, [2, T], BF16),
                            in1=qdec_t[:, hh, :].unsqueeze(1).broadcast_to([128, 2, T]), op=ALU.mult),
                            reads=[c1, c2, constB], writes=[qd[hh]])
            for tb in range(4):
                pk = bank()
                pkb = pk.ap.bitcast(BF16)
                op(PE, lambda: [nc.tensor.transpose(out=pkb[:, c * 128:(c + 1) * 128], in_=kT[c].ap[:, tb * 128:(tb + 1) * 128], identity=idb_t[:])
                                for c in range(8)][-1], reads=kT + [constB], writes=[pk])
                op(DVE, lambda: nc.vector.tensor_tensor(
                    out=ktok[tb].ap, in0=pkb.rearrange("p (a b) -> p a b", a=NH),
                    in1=kdec_t[:].rearrange("p (a b) -> p a b", a=NH)[:, :, tb:tb + 1].broadcast_to([128, NH, DK]), op=ALU.mult),
                    reads=[pk, constB], writes=[ktok[tb]])
            for nb in range(4):
                wb, wap = wnext(s_rin[j, 4 + nb], 4096)
                wv = wap.rearrange("p (k n) -> p k n", k=KC)
                for tb in range(4):
                    pv = bank()
                    op(PE, lambda: [nc.tensor.matmul(pv.ap, lhsT=h[kc].ap[:, tb * 128:(tb + 1) * 128], rhs=wv[:, kc, :], start=(kc == 0), stop=(kc == KC - 1))
                                    for kc in range(KC)][-1], reads=h + [wb], writes=[pv])
                    if nb == 0:
                        op(ACT, lambda: nc.scalar.activation(out=vtok[tb].ap[:, nb * 512:(nb + 1) * 512], in_=pv.ap, func=AF.Copy),
                           reads=[pv], writes=[vtok[tb]])
                    else:
                        ACT.wait(_gather([pv], []))
                        ev = ACT.sig(nc.scalar.activation(out=vtok[tb].ap[:, nb * 512:(nb + 1) * 512], in_=pv.ap, func=AF.Copy))
                        _commit(ev, [pv], [])
                        _mg(vtok[tb].w, ev[0], ev[1])
            for gb in range(4):
                wb, wap = wnext(s_rin[j, 8 + gb], 4096)
                wv = wap.rearrange("p (k n) -> p k n", k=KC)
                for c4 in range(4):
                    c = gb * 4 + c4
                    pg = bank()
                    op(PE, lambda: [nc.tensor.matmul(pg.ap, lhsT=wv[:, kc, c4 * 128:(c4 + 1) * 128], rhs=h[kc].ap, start=(kc == 0), stop=(kc == KC - 1))
                                    for kc in range(KC)][-1], reads=h + [wb], writes=[pg])
                    op(ACT, lambda: nc.scalar.activation(out=gs[c].ap, in_=pg.ap, func=AF.Silu), reads=[pg], writes=[gs[c]])
            preload_ln()
            W.sync()
            Pb = [[W.buf(q * 4096 + jb * 1024, [T], BF16) for jb in range(4)] for q in range(2)]
            o32_ = [[W.buf(8192 + q * 8192 + vc * 2048, [T], F32) for vc in range(4)] for q in range(2)]
            sqo_ = [[W.buf(24576 + q * 4096 + vc * 1024, [T], BF16) for vc in range(4)] for q in range(2)]
            r1 = W.buf(32768, [T], F32)
            rs = W.buf(34816, [T], F32)
            to = [W.buf(102400 + q * 2048, [T], F32) for q in range(2)]
            Sst = SB_[j]

            def scores(hh):
                P_ = Pb[hh % 2]
                for jb in range(4):
                    n = (4 - jb) * 128
                    pss = bank()
                    op(PE, lambda: [nc.tensor.matmul(pss.ap[:, 0:n], lhsT=kT[hh * 2 + dc].ap[:, jb * 128:(jb + 1) * 128],
                                                     rhs=qT[hh * 2 + dc].ap[:, jb * 128:T], start=(dc == 0), stop=(dc == 1))
                                    for dc in range(2)][-1], reads=[kT[hh * 2], kT[hh * 2 + 1], qT[hh * 2], qT[hh * 2 + 1]], writes=[pss])
                    op(DVE, lambda: nc.vector.tensor_tensor(out=P_[jb].ap[:, 0:n], in0=pss.ap[:, 0:n], in1=mask_t[:, hh, 0:n], op=ALU.mult),
                       reads=[pss, constB], writes=[P_[jb]])

            scores(0)
            for hh in range(NH):
                P_ = Pb[hh % 2]
                o32 = o32_[hh % 2]
                sqo = sqo_[hh % 2]
                for vc in range(4):
                    po = bank()

                    def mm_o():
                        last = None
                        if not first:
                            for dc in range(2):
                                last = nc.tensor.matmul(po.ap, lhsT=Sst[hh].ap[:, dc, vc * 128:(vc + 1) * 128], rhs=qd[hh].ap[:, dc, :],
                                                        start=(dc == 0), stop=False)
                        for jb in range(4):
                            n = (4 - jb) * 128
                            last = nc.tensor.matmul(po.ap[:, jb * 128:T], lhsT=vtok[jb].ap[:, hh * DV + vc * 128:hh * DV + (vc + 1) * 128],
                                                    rhs=P_[jb].ap[:, 0:n], start=(first and jb == 0), stop=(jb == 3))
                        return last
                    op(PE, mm_o, reads=vtok + P_ + ([] if first else [Sst[hh], qd[hh]]), writes=[po])
                    op(ACT, lambda: nc.scalar.activation(out=o32[vc].ap, in_=po.ap, func=AF.Copy), reads=[po], writes=[o32[vc]])
                    op(ACT, lambda: nc.scalar.activation(out=sqo[vc].ap, in_=po.ap, func=AF.Square), reads=[po], writes=[sqo[vc]])
                    if vc == 1 and hh + 1 < NH:
                        scores(hh + 1)
                pst = bank()
                op(PE, lambda: [nc.tensor.matmul(pst.ap, lhsT=ones_t[:], rhs=sqo[vc].ap, start=(vc == 0), stop=(vc == 3))
                                for vc in range(4)][-1], reads=sqo + [constB], writes=[pst])
                op(ACT, lambda: nc.scalar.activation(out=r1.ap, in_=pst.ap, func=AF.Ln, bias=float(DV * EPS), scale=1.0),
                   reads=[pst], writes=[r1])
                op(ACT, lambda: nc.scalar.activation(out=rs.ap, in_=r1.ap, func=AF.Exp, scale=-0.5), reads=[r1], writes=[rs])
                for vc in range(4):
                    c = hh * 4 + vc
                    tob = to[vc % 2]
                    op(DVE, lambda: nc.vector.scalar_tensor_tensor(out=tob.ap, in0=o32[vc].ap, scalar=GN_t[:, j * 16 + c:j * 16 + c + 1],
                                                                   in1=rs.ap, op0=ALU.mult, op1=ALU.mult),
                       reads=[o32[vc], rs, modB], writes=[tob])
                    op(POOL, lambda: nc.gpsimd.tensor_tensor(out=gs[c].ap, in0=tob.ap, in1=gs[c].ap, op=ALU.mult),
                       reads=[tob], writes=[gs[c]])
            for hh in range(NH):
                for dc in range(2):
                    pd = bank()
                    op(PE, lambda: [nc.tensor.matmul(pd.ap, lhsT=ktok[jb].ap[:, hh, dc * 128:(dc + 1) * 128],
                                                     rhs=vtok[jb].ap[:, hh * DV:(hh + 1) * DV], start=(jb == 0), stop=(jb == 3))
                                    for jb in range(4)][-1], reads=ktok + vtok, writes=[pd])
                    if dc == 0:
                        if first:
                            op(ACT, lambda: nc.scalar.activation(out=Sst[hh].ap[:, dc, :], in_=pd.ap, func=AF.Copy), reads=[pd], writes=[Sst[hh]])
                        else:
                            op(DVE, lambda: nc.vector.scalar_tensor_tensor(out=Sst[hh].ap[:, dc, :], in0=Sst[hh].ap[:, dc, :],
                                                                           scalar=float(GAMMA[hh] ** T), in1=pd.ap, op0=ALU.mult, op1=ALU.add),
                               reads=[pd], writes=[Sst[hh]])
                    else:
                        e_ = ACT if first else DVE
                        e_.wait(_gather([pd], []))
                        if first:
                            inst = nc.scalar.activation(out=Sst[hh].ap[:, dc, :], in_=pd.ap, func=AF.Copy)
                        else:
                            inst = nc.vector.scalar_tensor_tensor(out=Sst[hh].ap[:, dc, :], in0=Sst[hh].ap[:, dc, :],
                                                                  scalar=float(GAMMA[hh] ** T), in1=pd.ap, op0=ALU.mult, op1=ALU.add)
                        ev = e_.sig(inst)
                        _commit(ev, [pd], [])
                        _mg(Sst[hh].w, ev[0], ev[1])
            for g in range(4):
                wb, wap = wnext(s_rout[j, g], 4096)
                wv = wap.rearrange("p (c n) -> p c n", c=16)
                for mi in range(2):
                    m = 2 * g + mi
                    py = bank()
                    mmgroup(py, [(py.ap, wv[:, c, mi * 128:(mi + 1) * 128], gs[c].ap, c == 0, c == 15, [gs[c]]) for c in range(16)], [wb])
                    resid(l, 1, b, m, py)

        def shortconv(l, b, first):
            j = l // 2
            W.reset()
            h = modulate(l, 1, b)
            R2 = 36864
            tcv = [W.buf(R2 + c * 2048, [T], F32) for c in range(8)]
            csb = [W.buf(R2 + 16384 + c * 2048, [T], F32) for c in range(8)]
            zb = [W.buf(R2 + 32768 + c * 2064, [T + 4], F32) for c in range(8)]
            W.sync()
            gc = [W.buf(c * 1024, [T], BF16) for c in range(8)]
            cwr = R_CW + j * 24
            for blk in (2, 3, 4, 5, 0, 1):
                wb, wap = wnext(s_cin[j, blk], 4096)
                wv = wap.rearrange("p (k n) -> p k n", k=KC)
                kind = blk // 2
                for c4 in range(4):
                    c = (blk % 2) * 4 + c4
                    pp = bank()
                    mmgroup(pp, [(pp.ap, wv[:, kc, c4 * 128:(c4 + 1) * 128], h[kc].ap, kc == 0, kc == KC - 1, [h[kc]]) for kc in range(KC)], [wb])
                    if kind == 1:
                        op(ACT, lambda: nc.scalar.activation(out=csb[c].ap, in_=pp.ap, func=AF.Copy), reads=[pp], writes=[csb[c]])
                    elif kind == 2:
                        z = zb[c]
                        t = tcv[c]
                        if first:
                            op(POOL, lambda: nc.gpsimd.memset(z.ap[:, 0:2], 0.0), writes=[z])
                        else:
                            op(POOL, lambda: nc.gpsimd.tensor_copy(out=z.ap[:, 0:2], in_=carry_t[:, j, c, :]), reads=[carryB[j]], writes=[z])
                        DVE.wait(_gather([pp, csb[c]], [z]))
                        ev = DVE.sig(nc.vector.tensor_tensor(out=z.ap[:, 2:T + 2], in0=pp.ap, in1=csb[c].ap, op=ALU.mult))
                        _commit(ev, [pp, csb[c]], [z])
                        op(ACT, lambda: nc.scalar.activation(out=t.ap, in_=z.ap[:, 2:T + 2], func=AF.Identity, bias=0.0,
                                                             scale=VT_t[:, cwr + 16 + c:cwr + 16 + c + 1]), reads=[z, VTB], writes=[t])
                        op(DVE, lambda: nc.vector.scalar_tensor_tensor(out=t.ap, in0=z.ap[:, 1:T + 1], scalar=VT_t[:, cwr + 8 + c:cwr + 8 + c + 1],
                                                                       in1=t.ap, op0=ALU.mult, op1=ALU.add), reads=[z, VTB], writes=[t])
                        op(DVE, lambda: nc.vector.scalar_tensor_tensor(out=t.ap, in0=z.ap[:, 0:T], scalar=VT_t[:, cwr + c:cwr + c + 1],
                                                                       in1=t.ap, op0=ALU.mult, op1=ALU.add), reads=[z, VTB], writes=[t])
                        if c == 0:
                            op(POOL, lambda: nc.gpsimd.tensor_copy(out=carry_t[:, j, c, :], in_=z.ap[:, T:T + 2]), reads=[z], writes=[carryB[j]])
                        else:
                            POOL.wait(_gather([z], []))
                            ev = POOL.sig(nc.gpsimd.tensor_copy(out=carry_t[:, j, c, :], in_=z.ap[:, T:T + 2]))
                            _commit(ev, [z], [])
                            _mg(carryB[j].w, ev[0], ev[1])
                    else:
                        op(DVE, lambda: nc.vector.tensor_tensor(out=gc[c].ap, in0=pp.ap, in1=tcv[c].ap, op=ALU.mult),
                           reads=[pp, tcv[c]], writes=[gc[c]])
            for blk in range(2):
                wb, wap = wnext(s_cout[j, blk], 4096)
                wv = wap.rearrange("p (k n) -> p k n", k=KC)
                for m4 in range(4):
                    m = blk * 4 + m4
                    py = bank()
                    mmgroup(py, [(py.ap, wv[:, c, m4 * 128:(m4 + 1) * 128], gc[c].ap, c == 0, c == KC - 1, [gc[c]]) for c in range(KC)], [wb])
                    resid(l, 1, b, m, py)

        def final_norm():
            W.reset()
            sq = [W.buf(kc * 1024, [T], BF16) for kc in range(KC)]
            r1 = W.buf(24576, [T], F32)
            rs = W.buf(26624, [T], F32)
            yT = [W.buf(36864 + kc * 2048, [T], F32) for kc in range(KC)]
            preload_ln()
            for kc in range(KC):
                op(ACT, lambda: nc.scalar.activation(out=sq[kc].ap, in_=xT_t[:, kc, :], func=AF.Square), reads=[xTB[kc]], writes=[sq[kc]])
            pst = bank()
            op(PE, lambda: [nc.tensor.matmul(pst.ap, lhsT=ones_t[:], rhs=sq[kc].ap, start=(kc == 0), stop=(kc == KC - 1))
                            for kc in range(KC)][-1], reads=sq + [constB], writes=[pst])
            op(ACT, lambda: nc.scalar.activation(out=r1.ap, in_=pst.ap, func=AF.Ln, bias=float(D * EPS), scale=1.0), reads=[pst], writes=[r1])
            op(ACT, lambda: nc.scalar.activation(out=rs.ap, in_=r1.ap, func=AF.Exp, scale=-0.5), reads=[r1], writes=[rs])
            for kc in range(KC):
                op(DVE, lambda: nc.vector.scalar_tensor_tensor(out=yT[kc].ap, in0=xT_t[:, kc, :], scalar=FG_t[:, kc:kc + 1], in1=rs.ap,
                                                               op0=ALU.mult, op1=ALU.mult), reads=[xTB[kc], rs, modB], writes=[yT[kc]])
            return yT

        for seq in range(NSEQ):
            for ti in range(NTILES):
                first = (ti == 0)
                t0 = ti * T
                dma(SP, sem_rope, rope_t[:], rope_d[:, :, t0:t0 + T], writes=[ropeB])
                W.reset()
                if xin_next[0] is None:
                    xioB = W.buf(XIN_OFF, [4, D], F32)
                    dma(SP, sem_xio, xioB.ap, x_d[seq, t0:t0 + T, :].rearrange("(tb p) d -> p tb d", p=128), writes=[xioB])
                else:
                    xioB = xin_next[0]
                    xin_next[0] = None
                    W.bufs.append(xioB)
                for kc in range(KC):
                    pb = bank()
                    op(PE, lambda: [nc.tensor.transpose(out=pb.ap[:, tb * 128:(tb + 1) * 128], in_=xioB.ap[:, tb, kc * 128:(kc + 1) * 128], identity=id32_t[:])
                                    for tb in range(4)][-1], reads=[xioB, constB], writes=[pb])
                    if kc % 2 == 0:
                        op(ACT, lambda: nc.scalar.activation(out=xT_t[:, kc, :], in_=pb.ap, func=AF.Copy), reads=[pb], writes=[xTB[kc]])
                    else:
                        op(DVE, lambda: nc.vector.tensor_copy(out=xT_t[:, kc, :], in_=pb.ap), reads=[pb], writes=[xTB[kc]])
                nxt = (seq, ti + 1) if ti + 1 < NTILES else ((seq + 1, 0) if seq + 1 < NSEQ else None)
                for l in layers:
                    ffn(l, 0, seq)
                    if l % 2 == 0:
                        retention(l, seq, first)
                    else:
                        shortconv(l, seq, first)
                    ffn(l, 1, seq, prefetch=(nxt if (l == layers[-1] and X_PREFETCH) else None))
                if do_final:
                    yT = final_norm()
                    ysrc = [(yT[kc].ap, yT[kc]) for kc in range(KC)]
                else:
                    W.reset()
                    ysrc = [(xT_t[:, kc, :], xTB[kc]) for kc in range(KC)]
                xioB = W.buf(XIO_OFF, [4, D], F32)
                xio_v = xioB.ap
                for tb in range(4):
                    for hf in range(2):
                        pb = bank()
                        op(PE, lambda: [nc.tensor.transpose(out=pb.ap[:, k4 * 128:(k4 + 1) * 128], in_=ysrc[hf * 4 + k4][0][:, tb * 128:(tb + 1) * 128],
                                                            identity=id32_t[:]) for k4 in range(4)][-1],
                           reads=[ysrc[hf * 4 + k4][1] for k4 in range(4)] + [constB], writes=[pb])
                        if tb == 0 and hf == 0:
                            op(ACT, lambda: nc.scalar.activation(out=xio_v[:, tb, hf * 512:(hf + 1) * 512], in_=pb.ap, func=AF.Copy), reads=[pb], writes=[xioB])
                        else:
                            e_ = ACT if (tb * 2 + hf) % 2 == 0 else DVE
                            e_.wait(_gather([pb], []))
                            if e_ is ACT:
                                inst = nc.scalar.activation(out=xio_v[:, tb, hf * 512:(hf + 1) * 512], in_=pb.ap, func=AF.Copy)
                            else:
                                inst = nc.vector.tensor_copy(out=xio_v[:, tb, hf * 512:(hf + 1) * 512], in_=pb.ap)
                            ev = e_.sig(inst)
                            _commit(ev, [pb], [])
                            _mg(xioB.w, ev[0], ev[1])
                dma(ACT if OUT_ON_ACT else SP, sem_out, y_d[seq, t0:t0 + T, :].rearrange("(tb p) d -> p tb d", p=128), xio_v, reads=[xioB])
        SP.wait({sem_out: dcnt[sem_out]})
        if dbg:
            print("instr counts:", {k: v.cnt for k, v in dict(PE=PE, ACT=ACT, DVE=DVE, POOL=POOL).items()},
                  "waits:", {k: v.nwait for k, v in dict(PE=PE, ACT=ACT, DVE=DVE, POOL=POOL, SP=SP).items()})
    return nc


def _const_tables(S):
    half = DK // 2
    inv_freq = (10000.0 ** (-np.arange(half, dtype=np.float64) * 2.0 / DK)).astype(np.float32)
    pos = np.arange(S, dtype=np.float32)
    ang = (inv_freq[:, None] * pos[None, :]).astype(np.float32)
    rope = np.stack([np.cos(ang.astype(np.float64)), np.sin(ang.astype(np.float64))], axis=1).astype(np.float32)
    tabs = np.zeros((128, NTAB), np.float32)
    jj = np.arange(128, dtype=np.float64)[:, None]
    ii = np.arange(T, dtype=np.float64)[None, :]
    for hh in range(NH):
        g = GAMMA[hh]
        dlt = ii - jj
        m = np.where(dlt >= 0, g ** np.maximum(dlt, 0.0), 0.0)
        tabs[:, hh * T:(hh + 1) * T] = m
        tabs[:, 2048 + hh * T:2048 + (hh + 1) * T] = (g ** (ii + 1.0))
        for jb in range(4):
            tabs[:, 4096 + hh * 4 + jb] = g ** (T - 1.0 - (jb * 128 + jj[:, 0]))
    tabs[:, 4112:4240] = np.eye(128, dtype=np.float32)
    return np.ascontiguousarray(rope), tabs


def _pack_vecs(c2, ada_b, norm_g, final_g, conv_w, ret_gn_g):
    v = np.zeros((NVROW, 128), np.float32)
    cr = c2.reshape(-1, 128)
    v[R_C:R_C + cr.shape[0]] = cr
    v[R_NG:R_NG + 96] = norm_g.reshape(96, 128)
    v[R_FG:R_FG + 8] = final_g.reshape(8, 128)
    v[R_CW:R_CW + 48] = conv_w.reshape(48, 128)
    v[R_GN:R_GN + 32] = ret_gn_g.reshape(32, 128)
    v[R_AB:R_AB + 288] = ada_b.reshape(288, 128)
    return v


_prog_cache = {}


def _get_prog(NSEQ, S, layers, do_final):
    key = (NSEQ, S, tuple(layers), do_final)
    if key not in _prog_cache:
        _prog_cache[key] = build_program(NSEQ, S, list(layers), do_final)
    return _prog_cache[key]


def run_layers(x, c, ada_w, ada_b, norm_g, ffn_w_in, ffn_w_out, ret_w_in, ret_gn_g, ret_w_out,
               conv_w_in, conv_w, conv_w_out, final_g, layers, do_final, ncores):
    B, S, _ = x.shape
    nseq = B // ncores
    rope, tabs = _const_tables(S)
    f = lambda a: np.ascontiguousarray(np.asarray(a, dtype=np.float32))
    shared = dict(tabs=tabs, rope=rope, ada_w=f(ada_w), ffn_w_in=f(ffn_w_in), ffn_w_out=f(ffn_w_out),
                  ret_w_in=f(ret_w_in), ret_w_out=f(ret_w_out), conv_w_in=f(conv_w_in), conv_w_out=f(conv_w_out))
    x = f(x)
    c = f(c)
    in_maps = []
    for i in range(ncores):
        m = dict(shared)
        m["x"] = np.ascontiguousarray(x[i * nseq:(i + 1) * nseq])
        m["vecs"] = _pack_vecs(c[i * nseq:(i + 1) * nseq], f(ada_b), f(norm_g), f(final_g), f(conv_w), f(ret_gn_g))
        in_maps.append(m)
    nc = _get_prog(nseq, S, layers, do_final)
    res = run_bass_kernel_spmd(nc, in_maps, core_ids=list(range(ncores)))
    return np.concatenate([r["y"] for r in res.results], axis=0)


def kernel(x, c, ada_w, ada_b, norm_g, ffn_w_in, ffn_w_out, ret_w_in, ret_gn_g, ret_w_out,
           conv_w_in, conv_w, conv_w_out, final_g):
    return run_layers(x, c, ada_w, ada_b, norm_g, ffn_w_in, ffn_w_out, ret_w_in, ret_gn_g, ret_w_out,
                      conv_w_in, conv_w, conv_w_out, final_g, layers=(0, 1, 2, 3), do_final=True, ncores=NCORES)
```
